# Optimizing a Trainium2 kernel written in Bass

```python
import jax, jax.numpy as jnp
from jax import lax
import numpy as np

D_MODEL = 2048
BATCH = 1
SEQ = 8192
DEPTH = 1

A_HEADS = 4
A_DK = 256
A_DV = 512
B_HEADS = 16
B_DK = 128
B_DV = 128
CONV_K = 5
CHUNK = 64
N_BRANCH = 2
D_FF = -(-8 * D_MODEL // (3 * 256)) * 256
EPS = 1e-6
ADA_SCALE = 0.5

A_QK = A_HEADS * A_DK
A_V = A_HEADS * A_DV
B_QKV = 2 * B_HEADS * B_DK + B_HEADS * B_DV
B_V = B_HEADS * B_DV
IN_SIZES = (
    A_QK,
    A_QK,
    A_V,
    A_V,
    2 * A_HEADS,
    2 * A_HEADS,
    B_QKV,
    B_V,
    2 * B_HEADS,
    2 * B_HEADS,
    N_BRANCH * D_MODEL,
)
IN_SPLITS = tuple(int(s) for s in np.cumsum(IN_SIZES)[:-1])
D_IN = int(sum(IN_SIZES))

kernel_name = 'hybrid_mlstm_gdn_bidir_block'


def rms_norm(x, w):
    xf = x.astype(jnp.float32)
    y = xf * lax.rsqrt(jnp.mean(xf * xf, axis=-1, keepdims=True) + EPS)
    return (y * w.astype(jnp.float32)).astype(x.dtype)


def l2_normalize(a):
    return a * lax.rsqrt(jnp.sum(a * a, axis=-1, keepdims=True) + EPS)


def to_chunks(a):
    b, h, t = a.shape[:3]
    a = a.reshape((b, h, t // CHUNK, CHUNK) + a.shape[3:])
    return jnp.moveaxis(a, 2, 0)


def from_chunks(a):
    a = jnp.moveaxis(a, 0, 2)
    b, h, n, l = a.shape[:4]
    return a.reshape((b, h, n * l) + a.shape[4:])


def mlstm_chunkwise(q, k, v, log_i, log_f):
    b, h, _, dk = q.shape
    dv = v.shape[-1]
    qc, kc, vc, lic = (to_chunks(a) for a in (q, k, v, log_i))
    fc = jnp.cumsum(to_chunks(log_f), axis=-1)
    lower = jnp.tril(jnp.ones((CHUNK, CHUNK), dtype=bool))

    def step(carry, inp):
        c_mem, n_mem, m_mem = carry
        q_, k_, v_, li, fcum = inp
        d_log = jnp.where(lower, fcum[..., :, None] - fcum[..., None, :] + li[..., None, :], -jnp.inf)
        inter_log = fcum + m_mem[..., None]
        m_t = jnp.maximum(inter_log, jnp.max(d_log, axis=-1))
        p = jnp.exp(d_log - m_t[..., None]) * jnp.einsum('bhtd,bhsd->bhts', q_, k_)
        w_inter = jnp.exp(inter_log - m_t)
        num = (w_inter[..., None] * jnp.einsum('bhtd,bhde->bhte', q_, c_mem)
               + jnp.einsum('bhts,bhse->bhte', p, v_))
        den = w_inter * jnp.einsum('bhtd,bhd->bht', q_, n_mem) + jnp.sum(p, axis=-1)
        h_t = num / jnp.maximum(jnp.abs(den), jnp.exp(-m_t))[..., None]
        f_tot = fcum[..., -1]
        a_log = f_tot[..., None] - fcum + li
        m_new = jnp.maximum(f_tot + m_mem, jnp.max(a_log, axis=-1))
        w_s = jnp.exp(a_log - m_new[..., None])
        carry_decay = jnp.exp(f_tot + m_mem - m_new)
        c_mem = carry_decay[..., None, None] * c_mem + jnp.einsum('bhs,bhsd,bhse->bhde', w_s, k_, v_)
        n_mem = carry_decay[..., None] * n_mem + jnp.einsum('bhs,bhsd->bhd', w_s, k_)
        return (c_mem, n_mem, m_new), h_t

    init = (jnp.zeros((b, h, dk, dv), jnp.float32),
            jnp.zeros((b, h, dk), jnp.float32),
            jnp.zeros((b, h), jnp.float32))
    _, hs = lax.scan(step, init, (qc, kc, vc, lic, fc))
    return from_chunks(hs)


def gated_delta_chunkwise(q, k, v, beta, g):
    b, h, _, dk = q.shape
    dv = v.shape[-1]
    qc, kc, vc, betac = (to_chunks(a) for a in (q, k, v, beta))
    gc = jnp.cumsum(to_chunks(g), axis=-1)
    lower = jnp.tril(jnp.ones((CHUNK, CHUNK), dtype=bool))
    strict = jnp.tril(jnp.ones((CHUNK, CHUNK), dtype=bool), -1)
    decay_lower = jnp.exp(jnp.where(lower, gc[..., :, None] - gc[..., None, :], -jnp.inf))
    decay_strict = jnp.where(strict, decay_lower, 0.0)
    a_mat = jnp.eye(CHUNK, dtype=jnp.float32) + (
        betac[..., :, None] * jnp.einsum('nbhid,nbhjd->nbhij', kc, kc) * decay_strict)
    t_mat = lax.linalg.triangular_solve(
        a_mat, jnp.broadcast_to(jnp.eye(CHUNK, dtype=jnp.float32), a_mat.shape),
        left_side=True, lower=True, unit_diagonal=True)
    u = jnp.einsum('nbhij,nbhje->nbhie', t_mat, betac[..., None] * vc)
    w = jnp.einsum('nbhij,nbhjd->nbhid', t_mat, (betac * jnp.exp(gc))[..., None] * kc)

    def step(s_mem, inp):
        q_, k_, u_, w_, g_, dmask = inp
        attn = jnp.einsum('bhid,bhjd->bhij', q_, k_) * dmask
        v_new = u_ - jnp.einsum('bhid,bhde->bhie', w_, s_mem)
        o = (jnp.einsum('bhid,bhde->bhie', q_ * jnp.exp(g_)[..., None], s_mem)
             + jnp.einsum('bhij,bhje->bhie', attn, v_new))
        g_last = g_[..., -1]
        s_mem = (jnp.exp(g_last)[..., None, None] * s_mem
                 + jnp.einsum('bhjd,bhje->bhde', k_ * jnp.exp(g_last[..., None] - g_)[..., None], v_new))
        return s_mem, o

    _, os_ = lax.scan(step, jnp.zeros((b, h, dk, dv), jnp.float32), (qc, kc, u, w, gc, decay_lower))
    return from_chunks(os_)


def bidirectional(fn, seq_inputs, gates_fwd, gates_bwd):
    rev = lambda a: jnp.flip(a, axis=2)
    out_f = fn(*seq_inputs, *gates_fwd)
    out_b = rev(fn(*(rev(a) for a in seq_inputs), *(rev(a) for a in gates_bwd)))
    return out_f + out_b


def centred_depthwise_conv(x, w):
    ch = x.shape[-1]
    return lax.conv_general_dilated(
        x, w[:, None, :].astype(x.dtype), window_strides=(1,),
        padding=[(CONV_K // 2, CONV_K // 2)],
        dimension_numbers=('NWC', 'WIO', 'NWC'), feature_group_count=ch)


def hybrid_layer(x, c_act, ada_w, ada_b, norm_mix_w, w_in, mlstm_b_i, mlstm_b_f, mlstm_norm_w,
                 gdn_conv_w, gdn_A_log, gdn_dt_bias, gdn_norm_w, w_branch_a, w_branch_b, w_out,
                 norm_ffn_w, w_gate_up, w_down):
    bsz, t, _ = x.shape
    f32 = jnp.float32
    mod = c_act @ ada_w + ada_b
    shift_m, scale_m, gate_m, shift_f, scale_f, gate_f = jnp.split(mod[:, None, :], 6, axis=-1)

    h = rms_norm(x, norm_mix_w) * (1 + scale_m) + shift_m
    proj = h @ w_in
    (a_q, a_k, a_v, a_o, a_i, a_f, b_qkv, b_z, b_beta, b_a, merge) = jnp.split(proj, IN_SPLITS, axis=-1)

    def heads(a, n, d):
        return a.reshape(bsz, t, n, d).transpose(0, 2, 1, 3).astype(f32)

    def dir_heads(a, n):
        return a.reshape(bsz, t, 2, n).astype(f32).transpose(2, 0, 3, 1)

    q_a = heads(a_q, A_HEADS, A_DK) * (A_DK ** -0.5)
    k_a = heads(a_k, A_HEADS, A_DK)
    v_a = heads(a_v, A_HEADS, A_DV)
    log_i = dir_heads(a_i, A_HEADS) + mlstm_b_i.astype(f32)[:, None, :, None]
    log_f = jax.nn.log_sigmoid(dir_heads(a_f, A_HEADS) + mlstm_b_f.astype(f32)[:, None, :, None])
    h_a = bidirectional(mlstm_chunkwise, (q_a, k_a, v_a), (log_i[0], log_f[0]), (log_i[1], log_f[1]))
    h_a = rms_norm(h_a.transpose(0, 2, 1, 3), mlstm_norm_w.reshape(A_HEADS, A_DV)).reshape(bsz, t, A_V)
    y_a = (jax.nn.sigmoid(a_o.astype(f32)) * h_a).astype(x.dtype) @ w_branch_a

    qkv = jax.nn.silu(centred_depthwise_conv(b_qkv, gdn_conv_w))
    b_q, b_k, b_v = jnp.split(qkv, [B_HEADS * B_DK, 2 * B_HEADS * B_DK], axis=-1)
    q_b = l2_normalize(heads(b_q, B_HEADS, B_DK)) * (B_DK ** -0.5)
    k_b = l2_normalize(heads(b_k, B_HEADS, B_DK))
    v_b = heads(b_v, B_HEADS, B_DV)
    beta = jax.nn.sigmoid(dir_heads(b_beta, B_HEADS))
    log_decay = -jnp.exp(gdn_A_log.astype(f32))[:, None, :, None] * jax.nn.softplus(
        dir_heads(b_a, B_HEADS) + gdn_dt_bias.astype(f32)[:, None, :, None])
    o_b = bidirectional(gated_delta_chunkwise, (q_b, k_b, v_b), (beta[0], log_decay[0]), (beta[1], log_decay[1]))
    o_b = rms_norm(o_b.transpose(0, 2, 1, 3), gdn_norm_w.reshape(B_HEADS, B_DV)).reshape(bsz, t, B_V)
    y_b = (o_b * jax.nn.silu(b_z.astype(f32))).astype(x.dtype) @ w_branch_b

    g_a, g_b = jnp.split(jax.nn.sigmoid(merge), N_BRANCH, axis=-1)
    x = x + gate_m * ((g_a * y_a + g_b * y_b) @ w_out)

    hf = rms_norm(x, norm_ffn_w) * (1 + scale_f) + shift_f
    gt, up = jnp.split(hf @ w_gate_up, 2, axis=-1)
    x = x + gate_f * ((jax.nn.silu(gt) * up) @ w_down)
    return x


def setup_inputs(seed: int = 0) -> dict:
    key = jax.random.key(seed)
    ks = jax.random.split(key, 24)
    f32 = jnp.float32
    nrm = lambda k, shape, s: jax.random.normal(k, shape, f32) * s
    gain = lambda k, shape: 1.0 + 0.02 * jax.random.normal(k, shape, f32)
    dt = jnp.exp(jax.random.uniform(ks[10], (DEPTH, 2, B_HEADS), f32, np.log(1e-3), np.log(1e-1)))
    return {
        'x': nrm(ks[0], (BATCH, SEQ, D_MODEL), 1.0),
        'c': nrm(ks[1], (BATCH, D_MODEL), 1.0),
        'ada_w': nrm(ks[2], (DEPTH, D_MODEL, 6 * D_MODEL), ADA_SCALE * D_MODEL ** -0.5),
        'ada_b': nrm(ks[3], (DEPTH, 6 * D_MODEL), 0.02),
        'norm_mix_w': gain(ks[4], (DEPTH, D_MODEL)),
        'w_in': nrm(ks[5], (DEPTH, D_MODEL, D_IN), D_MODEL ** -0.5),
        'mlstm_b_i': nrm(ks[6], (DEPTH, 2, A_HEADS), 0.1),
        'mlstm_b_f': jnp.linspace(3.0, 6.0, A_HEADS, dtype=f32) + nrm(ks[7], (DEPTH, 2, A_HEADS), 0.1),
        'mlstm_norm_w': gain(ks[8], (DEPTH, A_V)),
        'gdn_conv_w': nrm(ks[9], (DEPTH, CONV_K, B_QKV), CONV_K ** -0.5),
        'gdn_A_log': jnp.log(jax.random.uniform(ks[11], (DEPTH, 2, B_HEADS), f32, 1.0, 16.0)),
        'gdn_dt_bias': dt + jnp.log(-jnp.expm1(-dt)),
        'gdn_norm_w': gain(ks[12], (DEPTH, B_V)),
        'w_branch_a': nrm(ks[13], (DEPTH, A_V, D_MODEL), A_V ** -0.5),
        'w_branch_b': nrm(ks[14], (DEPTH, B_V, D_MODEL), B_V ** -0.5),
        'w_out': nrm(ks[15], (DEPTH, D_MODEL, D_MODEL), D_MODEL ** -0.5),
        'norm_ffn_w': gain(ks[16], (DEPTH, D_MODEL)),
        'w_gate_up': nrm(ks[17], (DEPTH, D_MODEL, 2 * D_FF), D_MODEL ** -0.5),
        'w_down': nrm(ks[18], (DEPTH, D_FF, D_MODEL), D_FF ** -0.5),
        'norm_final_w': gain(ks[19], (D_MODEL,)),
    }


def reference(x, c, ada_w, ada_b, norm_mix_w, w_in, mlstm_b_i, mlstm_b_f, mlstm_norm_w, gdn_conv_w,
              gdn_A_log, gdn_dt_bias, gdn_norm_w, w_branch_a, w_branch_b, w_out, norm_ffn_w,
              w_gate_up, w_down, norm_final_w):
    c_act = jax.nn.silu(c)
    for l in range(DEPTH):
        x = hybrid_layer(x, c_act, ada_w[l], ada_b[l], norm_mix_w[l], w_in[l], mlstm_b_i[l], mlstm_b_f[l],
                         mlstm_norm_w[l], gdn_conv_w[l], gdn_A_log[l], gdn_dt_bias[l], gdn_norm_w[l],
                         w_branch_a[l], w_branch_b[l], w_out[l], norm_ffn_w[l], w_gate_up[l], w_down[l])
    return rms_norm(x, norm_final_w)
```

```python
from contextlib import ExitStack
import numpy as np
import concourse.bass as bass
import concourse.mybir as mybir
from concourse.bass_utils import run_bass_kernel_spmd

F32 = mybir.dt.float32
BF16 = mybir.dt.bfloat16
ALU = mybir.AluOpType
AF = mybir.ActivationFunctionType
AX = mybir.AxisListType

D = 2048
T = 8192
NCORE = 8
KC = D // 128
EPS = 1e-6
A_DK = 256
B_DK = 128
NBLK = T // 128
NGRP = T // 512
import os
DBG = int(os.environ.get('DBG', '99'))
SKIP_SELF = int(os.environ.get('SKIP_SELF', '0'))
USE_F32R = int(os.environ.get('F32R', '1'))
F32R = mybir.dt.float32r if USE_F32R else F32


class Buf:
    __slots__ = ("name", "w", "r", "excl")

    def __init__(self, name="", excl=False):
        self.name = name
        self.w = None
        self.r = []
        self.excl = excl


ENGS = ("pe", "act", "dve", "pool", "sp")
EPOCH = 24000
NDMA = 24


class Prog:
    def __init__(self, nc, sems):
        self.nc = nc
        self.free = list(sems)
        self.ins = {e: [] for e in ENGS}
        self.cnt = {e: 0 for e in ENGS}
        self.sem = {e: self.free.pop() for e in ENGS}
        self.waited = {e: {} for e in ENGS}
        self.dsem = [self.free.pop() for _ in range(NDMA)]
        self.dval = [0] * NDMA
        self.dnext = 0
        self.semobj = {}
        self.own = {}
        for e in ENGS:
            self.semobj[id(self.sem[e])] = self.sem[e]
            self.own[id(self.sem[e])] = e
        for s in self.dsem:
            self.semobj[id(s)] = s

    def _need(self, eng, toks):
        best = {}
        for t in toks:
            if t is None:
                continue
            s, v = t
            k = id(s)
            if SKIP_SELF and eng in ("pe", "act", "dve", "pool") and self.own.get(k) == eng:
                continue
            if self.waited[eng].get(k, 0) >= v:
                continue
            if best.get(k, 0) < v:
                best[k] = v
        out = []
        for k, v in best.items():
            self.waited[eng][k] = v
            out.append((self.semobj[k], v))
        return out

    @staticmethod
    def _deps(reads, writes):
        toks = []
        for b in reads:
            toks.append(b.w)
        for b in writes:
            toks.append(b.w)
            toks.extend(b.r)
        return toks

    @staticmethod
    def _mark(tok, reads, writes):
        for b in reads:
            b.r.append(tok)
        for b in writes:
            b.w = tok
            b.r = []

    def op(self, eng, fns, reads=(), writes=()):
        if callable(fns):
            fns = [fns]
        if any(b.excl for b in reads):
            writes = list(writes) + [b for b in reads if b.excl]
            reads = [b for b in reads if not b.excl]
        waits = self._need(eng, self._deps(reads, writes))
        if self.cnt[eng] >= EPOCH:
            s = self.free.pop()
            self.semobj[id(s)] = s
            self.own[id(s)] = eng
            self.sem[eng] = s
            self.cnt[eng] = 0
        self.cnt[eng] += 1
        tok = (self.sem[eng], self.cnt[eng])
        self.ins[eng].append((waits, fns, (self.sem[eng], 1)))
        self._mark(tok, reads, writes)
        return tok

    def dma(self, q, fn, reads=(), writes=()):
        j = self.dnext
        self.dnext = (self.dnext + 1) % NDMA
        s = self.dsem[j]
        toks = self._deps(reads, writes)
        if self.dval[j] > 0:
            toks.append((s, self.dval[j]))
        waits = self._need(q, toks)
        self.dval[j] += 16
        tok = (s, self.dval[j])
        self.ins[q].append((waits, [fn], (s, 16)))
        self._mark(tok, reads, writes)
        return tok

    def wait_all(self, eng, bufs):
        waits = self._need(eng, [b.w for b in bufs])
        self.ins[eng].append((waits, [], None))

    def emit(self):
        nc = self.nc
        ins = self.ins
        if not any(ins[e] for e in ENGS):
            return
        with nc.Block() as block:
            def body(ename):
                def f(e):
                    for waits, fns, inc in ins[ename]:
                        for s, v in waits:
                            e.wait_ge(s, v)
                        last = None
                        for fn in fns:
                            last = fn(e)
                        if inc is not None and last is not None:
                            last.then_inc(inc[0], inc[1])
                return f
            block.tensor(body("pe"))
            block.scalar(body("act"))
            block.vector(body("dve"))
            block.gpsimd(body("pool"))
            block.sync(body("sp"))
        self.ins = {e: [] for e in ENGS}


class Ctx:
    n = 0

    def __init__(self, nc, es):
        self.nc = nc
        self.es = es

    def sb(self, shape, dt, name=None):
        Ctx.n += 1
        nm = f"{name or 'sb'}_{Ctx.n}"
        t = self.es.enter_context(self.nc.sbuf_tensor(nm, list(shape), dt))
        return t, Buf(nm)

    def ps(self, shape, dt, name=None):
        Ctx.n += 1
        nm = f"{name or 'ps'}_{Ctx.n}"
        t = self.es.enter_context(self.nc.psum_tensor(nm, list(shape), dt))
        return t, Buf(nm, excl=True)


def make_identity(P, C, dt=BF16):
    idf, b_idf = C.sb([128, 128], F32)
    ident, b_id = C.sb([128, 128], dt)
    P.op("pool", lambda e: e.memset(idf[:], 0.0), writes=[b_idf])
    P.op("pool", lambda e: e.affine_select(out=idf[:], in_=idf[:], pattern=[[-1, 128]],
                                           compare_op=ALU.not_equal, fill=1.0, base=0, channel_multiplier=1),
         reads=[b_idf], writes=[b_idf])
    P.op("dve", lambda e: e.tensor_copy(out=ident[:], in_=idf[:]), reads=[b_idf], writes=[b_id])
    return ident, b_id, idf, b_idf


def compute_mod_cols(P, C, nc, c_d, adaw_d, adab_d, ncols, mod_sb, b_mod):
    nj = ncols // 128
    c_sb, b_c = C.sb([128, KC], F32)
    cact, b_cact = C.sb([128, KC], BF16)
    ab, b_ab = C.sb([128, nj], F32)
    pm, b_pm = C.ps([128, nj], F32)
    P.dma("sp", lambda e: e.dma_start(out=c_sb[:], in_=c_d.rearrange("(k p) -> p k", p=128), allow_slow_non_contiguous=True), writes=[b_c])
    P.dma("sp", lambda e: e.dma_start(out=ab[:], in_=adab_d.rearrange("(j p) -> p j", p=128), allow_slow_non_contiguous=True), writes=[b_ab])
    P.op("act", lambda e: e.activation(out=cact[:], in_=c_sb[:], func=AF.Silu), reads=[b_c], writes=[b_cact])
    CH = 1024
    wbuf = [C.sb([128, KC, CH], BF16) for _ in range(2)]
    for ci in range(ncols // CH):
        wt, b_wt = wbuf[ci % 2]
        P.dma("pool", lambda e, wt=wt, ci=ci: e.dma_start(
            out=wt[:], in_=adaw_d[:, ci * CH:(ci + 1) * CH].rearrange("(k p) n -> p k n", p=128)), writes=[b_wt])
        for jj in range(CH // 128):
            j = ci * (CH // 128) + jj
            P.op("pe", [(lambda e, wt=wt, jj=jj, j=j, k=k: e.matmul(
                pm[:, j:j + 1], lhsT=wt[:, k, jj * 128:(jj + 1) * 128], rhs=cact[:, k:k + 1],
                start=(k == 0), stop=(k == KC - 1))) for k in range(KC)],
                reads=[b_wt, b_cact], writes=[b_pm])
    P.op("dve", lambda e: e.tensor_tensor(out=mod_sb[:, 0:nj], in0=pm[:], in1=ab[:], op=ALU.add),
         reads=[b_pm, b_ab], writes=[b_mod])


NF = 1280
NT = 268


def build_A(ngrp=NGRP, sh=None):
    nc = sh["nc"] if sh else bass.Bass("TRN2", target_bir_lowering=False)
    dt_in = lambda name, shape: nc.dram_tensor(name, shape, F32, kind="ExternalInput").ap()
    x_d = dt_in("x", [T, D])
    c_d = dt_in("c", [D])
    adaw_d = dt_in("ada_w2", [D, 2 * D + 1024])
    adab_d = dt_in("ada_b2", [2 * D + 1024])
    modout_d = nc.dram_tensor("modout", [128, 40], F32, kind="ExternalOutput").ap()
    nw_d = dt_in("norm_mix_w", [D])
    w1f_d = dt_in("w1f", [D, NF])
    w1t_d = dt_in("w1t", [D, NT])
    gbias_d = dt_in("gate_bias", [12])
    gA_d = dt_in("gate_A", [4])
    cw_d = dt_in("conv_w", [768, 5])
    def out(name, shape, dt):
        ap = nc.dram_tensor(name, shape, dt, kind=("Internal" if sh else "ExternalOutput")).ap()
        if sh:
            sh["scr"][name] = ap
        return ap
    qta_d = out("qta", [256, T], BF16)
    kta_d = out("kta", [256, T], BF16)
    ka_d = out("ka", [T, 256], BF16)
    va_d = out("va", [T, 256], BF16)
    qtb_d = out("qtb", [256, T], BF16)
    ktb_d = out("ktb", [256, T], BF16)
    kb_d = out("kb", [T, 256], BF16)
    vb_d = out("vb", [T, 256], BF16)
    gts_d = out("gts", [T, 12], F32)
    outbufs = []

    with ExitStack() as es0:
        if sh:
            P = sh["P"]
        else:
            sems = [es0.enter_context(nc.semaphore(f"s{i}")) for i in range(100)]
            P = Prog(nc, sems)
        C0 = Ctx(nc, es0)
        sc, b_sc = C0.sb([128, KC], F32, "sc")
        sh, b_sh = C0.sb([128, KC], F32, "sh")
        with ExitStack() as es1:
            C = Ctx(nc, es1)
            mod, b_mod = C.sb([128, 40], F32)
            nw, b_nw = C.sb([128, KC], F32)
            P.dma("sp", lambda e: e.dma_start(out=nw[:], in_=nw_d.rearrange("(k p) -> p k", p=128), allow_slow_non_contiguous=True), writes=[b_nw])
            compute_mod_cols(P, C, nc, c_d, adaw_d, adab_d, 2 * D + 1024, mod, b_mod)
            ob = Buf()
            outbufs.append(ob)
            P.dma("sp", lambda e: e.dma_start(out=modout_d, in_=mod[:]), reads=[b_mod], writes=[ob])
            P.op("dve", lambda e: e.scalar_tensor_tensor(out=sc[:], in0=mod[:, 16:32], scalar=1.0, in1=nw[:],
                                                         op0=ALU.add, op1=ALU.mult),
                 reads=[b_mod, b_nw], writes=[b_sc])
            P.op("dve", lambda e: e.tensor_copy(out=sh[:], in_=mod[:, 0:16]), reads=[b_mod], writes=[b_sh])
            P.wait_all("sp", [ob])
            P.emit()
        with ExitStack() as es2:
            C = Ctx(nc, es2)
            ident, b_id, idf, b_idf = make_identity(P, C)
            ones32, b_ones32 = C.sb([128, 128], F32)
            ones_f, b_ones = C.sb([128, 128], F32R)
            P.op("pool", lambda e: e.memset(ones32[:], 1.0), writes=[b_ones32])
            P.op("dve", lambda e: e.tensor_copy(out=ones_f[:], in_=ones32[:]), reads=[b_ones32], writes=[b_ones])
            w1f, b_w1f = C.sb([128, KC, NF], BF16)
            w1t, b_w1t = C.sb([128, KC, NT], BF16)
            for q in range(4):
                P.dma("pool", lambda e, q=q: e.dma_start(
                    out=w1f[:, :, q * 320:(q + 1) * 320],
                    in_=w1f_d[:, q * 320:(q + 1) * 320].rearrange("(k p) n -> p k n", p=128)), writes=[b_w1f])
            P.dma("pool", lambda e: e.dma_start(out=w1t[:], in_=w1t_d.rearrange("(k p) n -> p k n", p=128)),
                  writes=[b_w1t])
            cw, b_cw = C.sb([128, 6, 5], F32)
            P.dma("sp", lambda e: e.dma_start(out=cw[:], in_=cw_d.rearrange("(m p) k -> p m k", p=128)), writes=[b_cw])
            gb, b_gb = C.sb([128, 12], F32)
            P.dma("sp", lambda e: e.dma_start(out=gb[:], in_=gbias_d.partition_broadcast(128)), writes=[b_gb])
            negA, b_negA = C.sb([128, 4], F32)
            P.dma("sp", lambda e: e.dma_start(out=negA[:], in_=gA_d.partition_broadcast(128)), writes=[b_negA])
            P.op("act", lambda e: e.activation(out=negA[:], in_=negA[:], func=AF.Exp), reads=[b_negA], writes=[b_negA])
            P.op("dve", lambda e: e.tensor_scalar(out=negA[:], in0=negA[:], scalar1=-1.0, scalar2=None, op0=ALU.mult),
                 reads=[b_negA], writes=[b_negA])

            NXB = 3
            xbuf = [C.sb([128, D], F32) for _ in range(NXB)]
            junk, b_junk = C.sb([128, D], BF16)
            stat, b_stat = C.sb([128, NBLK, 3], F32)
            xsb = [C.sb([128, D], BF16) for _ in range(2)]
            hT = [C.sb([128, KC, 512], BF16) for _ in range(2)]
            pT, b_pT = C.ps([128, KC, 128], BF16)
            pF = [C.ps([128, 512], F32) for _ in range(2)]
            pTm = [C.ps([128, 512], F32) for _ in range(2)]
            pX = [C.ps([128, 1024], BF16) for _ in range(2)]
            raw2 = [[C.sb([128, 516], F32) for _ in range(6)] for _ in range(2)]
            for pp_ in range(2):
                for m in range(6):
                    P.op("pool", lambda e, t=raw2[pp_][m][0]: e.memset(t[:], 0.0), writes=[raw2[pp_][m][1]])
            acc = [C.sb([128, 512], F32) for _ in range(6)]
            sil = [C.sb([128, 512], F32) for _ in range(6)]
            rn = [C.sb([128, 512], F32) for _ in range(4)]
            sqr = [C.sb([128, 512], F32R) for _ in range(4)]
            fo = [C.sb([128, 512], BF16) for _ in range(12)]
            fk = [[C.sb([128, 512], BF16) for _ in range(2)] for _ in range(2)]
            to = [C.sb([128, 4, 128], BF16) for _ in range(4)]
            vo = [C.sb([128, 256], BF16) for _ in range(4)]
            z4, b_z4 = C.sb([128, 4, 12], F32)
            e4, b_e4 = C.sb([128, 4, 10], F32)
            g4 = [C.sb([128, 4, 12], F32) for _ in range(2)]
            cnt = {"fo": 0, "to": 0, "pF": 0, "pX": 0, "acc": 0}

            def rr(key, lst):
                i = cnt[key]
                cnt[key] += 1
                return lst[i % len(lst)]

            def load_x(n):
                xt, b_xt = xbuf[n % NXB]
                P.dma("sp", lambda e: e.dma_start(out=xt[:], in_=x_d[n * 128:(n + 1) * 128, :]), writes=[b_xt])

            def norm_block(n, hTg, b_hTg):
                xt, b_xt = xbuf[n % NXB]
                xs, b_xs = xsb[n % 2]
                P.op("act", lambda e: e.activation(out=junk[:], in_=xt[:], func=AF.Square,
                                                   accum_out=stat[:, n, 0:1]),
                     reads=[b_xt], writes=[b_junk, b_stat])
                P.op("act", lambda e: e.activation(out=stat[:, n, 1:2], in_=stat[:, n, 0:1], func=AF.Ln,
                                                   scale=1.0 / D, bias=EPS), reads=[b_stat], writes=[b_stat])
                P.op("act", lambda e: e.activation(out=stat[:, n, 2:3], in_=stat[:, n, 1:2], func=AF.Exp, scale=-0.5),
                     reads=[b_stat], writes=[b_stat])
                P.op("dve", lambda e: e.tensor_scalar(out=xs[:], in0=xt[:], scalar1=stat[:, n, 2:3], scalar2=None,
                                                      op0=ALU.mult), reads=[b_stat, b_xt], writes=[b_xs])
                P.op("pe", [(lambda e, k=k: e.transpose(out=pT[:, k, :], in_=xs[:, k * 128:(k + 1) * 128],
                                                        identity=ident[:])) for k in range(KC)],
                     reads=[b_xs, b_id], writes=[b_pT])
                o = (n % 4) * 128
                P.op("act", [(lambda e, k=k: e.activation(out=hTg[:, k, o:o + 128], in_=pT[:, k, :], func=AF.Identity,
                                                          scale=sc[:, k:k + 1], bias=sh[:, k:k + 1]))
                             for k in range(KC)],
                     reads=[b_pT, b_sc, b_sh], writes=[b_hTg])

            def store_fm(dst, rows, tlo, src_t, jlo, jhi, b_src):
                ob = Buf()
                outbufs.append(ob)
                P.dma("sp", lambda e: e.dma_start(out=dst[rows[0]:rows[1], tlo + jlo:tlo + jhi],
                                                    in_=src_t[:, jlo:jhi]), reads=[b_src], writes=[ob])

            def tm_from_fm(src_t, b_src, dst, col0, tlo, jlo, jhi):
                px, b_px = rr("pX", pX)
                tt, b_tt = rr("to", to)
                P.op("pe", [(lambda e, i=i: e.transpose(out=px[:, i * 128:(i + 1) * 128],
                                                        in_=src_t[:, i * 128:(i + 1) * 128], identity=ident[:]))
                            for i in range(4)], reads=[b_src, b_id], writes=[b_px])
                P.op("dve", lambda e: e.tensor_copy(out=tt[:].rearrange("p a b -> p (a b)"), in_=px[:, 0:512]),
                     reads=[b_px], writes=[b_tt])
                if jlo == 0 and jhi == 512:
                    ob = Buf()
                    outbufs.append(ob)
                    P.dma("sp", lambda e: e.dma_start(
                        out=dst[tlo:tlo + 512, col0:col0 + 128].rearrange("(i p) c -> p i c", p=128), in_=tt[:]),
                        reads=[b_tt], writes=[ob])
                    return
                for i in range(4):
                    lo = max(jlo, i * 128)
                    hi = min(jhi, (i + 1) * 128)
                    if lo >= hi:
                        continue
                    ob = Buf()
                    outbufs.append(ob)
                    P.dma("sp", lambda e, i=i, lo=lo, hi=hi: e.dma_start(
                        out=dst[tlo + lo:tlo + hi, col0:col0 + 128],
                        in_=tt[lo - i * 128:hi - i * 128, i, :]), reads=[b_tt], writes=[ob])

            def sigmoid_of(src, b_src, dst, b_dst):
                P.op("act", lambda e: e.activation(out=dst[:], in_=src[:], func=AF.Exp, scale=-1.0), reads=[b_src], writes=[b_dst])
                P.op("act", lambda e: e.activation(out=dst[:], in_=dst[:], func=AF.Ln, bias=1.0), reads=[b_dst], writes=[b_dst])
                P.op("act", lambda e: e.activation(out=dst[:], in_=dst[:], func=AF.Exp, scale=-1.0), reads=[b_dst], writes=[b_dst])

            def tm_gen(src_t, b_src, dst, col0, tlo, jlo, jhi):
                yield
                tm_from_fm(src_t, b_src, dst, col0, tlo, jlo, jhi)
                yield

            def gdn_post(g, m):
                rw, b_rw = raw2[g % 2][m]
                tlo = g * 512 - 2
                jlo = 2 if g == 0 else 0
                jhi = 2 if g == ngrp else 512
                ac, b_ac = acc[m]
                sl, b_sl = sil[m]
                P.op("dve", lambda e: e.tensor_scalar(out=ac[:], in0=rw[:, 0:512], scalar1=cw[:, m, 0:1], scalar2=None,
                                                      op0=ALU.mult), reads=[b_rw, b_cw], writes=[b_ac])
                for k in range(1, 5):
                    P.op("dve", lambda e, k=k: e.scalar_tensor_tensor(out=ac[:], in0=rw[:, k:k + 512],
                                                                      scalar=cw[:, m, k:k + 1], in1=ac[:],
                                                                      op0=ALU.mult, op1=ALU.add),
                         reads=[b_rw, b_cw, b_ac], writes=[b_ac])
                yield
                sigmoid_of(ac, b_ac, sl, b_sl)
                yield
                head = m % 2
                if m >= 4:
                    f, b_f = rr("fo", fo)
                    P.op("dve", lambda e: e.tensor_tensor(out=f[:], in0=ac[:], in1=sl[:], op=ALU.mult),
                         reads=[b_ac, b_sl], writes=[b_f])
                    yield
                    tm_from_fm(f, b_f, vb_d, head * 128, tlo, jlo, jhi)
                    yield
                    return
                P.op("dve", lambda e: e.tensor_tensor(out=sl[:], in0=ac[:], in1=sl[:], op=ALU.mult),
                     reads=[b_ac, b_sl], writes=[b_sl])
                s2t, b_s2 = sqr[m]
                s2 = s2t[:]
                r2, b_r2 = rn[m]
                P.op("dve", lambda e: e.tensor_tensor(out=s2, in0=sl[:], in1=sl[:], op=ALU.mult),
                     reads=[b_sl], writes=[b_s2])
                yield
                pf, b_pf = rr("pF", pF)
                P.op("pe", lambda e: e.matmul(pf[:], lhsT=ones_f[:], rhs=s2, start=True, stop=True),
                     reads=[b_s2, b_ones], writes=[b_pf])
                P.op("act", lambda e: e.activation(out=r2[:], in_=pf[:], func=AF.Ln, bias=EPS), reads=[b_pf], writes=[b_r2])
                P.op("act", lambda e: e.activation(out=r2[:], in_=r2[:], func=AF.Exp, scale=-0.5), reads=[b_r2], writes=[b_r2])
                yield
                f, b_f = rr("fo", fo)
                qs = (B_DK ** -0.5) if m < 2 else 1.0
                P.op("dve", lambda e: e.scalar_tensor_tensor(out=f[:], in0=sl[:], scalar=qs, in1=r2[:],
                                                             op0=ALU.mult, op1=ALU.mult),
                     reads=[b_sl, b_r2], writes=[b_f])
                dst = qtb_d if m < 2 else ktb_d
                store_fm(dst, (head * 128, head * 128 + 128), tlo, f, jlo, jhi, b_f)
                yield
                if m >= 2:
                    tm_from_fm(f, b_f, kb_d, head * 128, tlo, jlo, jhi)
                yield

            def unit_norm(n):
                g_ = n // 4
                hTg, b_hTg = hT[g_ % 2]
                norm_block(n, hTg, b_hTg)
                if n + NXB < 4 * ngrp:
                    load_x(n + NXB)

            pending = []

            def unit_feat(g, mb):
                hTg, b_hTg = hT[g % 2]
                pf, b_pf = rr("pF", pF)
                P.op("pe", [(lambda e, k=k: e.matmul(
                    pf[:], lhsT=w1f[:, k, mb * 128:(mb + 1) * 128], rhs=hTg[:, k, :],
                    start=(k == 0), stop=(k == KC - 1))) for k in range(KC)],
                    reads=[b_w1f, b_hTg], writes=[b_pf])
                if mb < 4:
                    f, b_f = rr("fo", fo) if mb < 2 else fk[g % 2][mb - 2]
                    if mb < 2:
                        P.op("act", lambda e: e.activation(out=f[:], in_=pf[:], func=AF.Copy, scale=A_DK ** -0.5),
                             reads=[b_pf], writes=[b_f])
                        store_fm(qta_d, (mb * 128, mb * 128 + 128), g * 512, f, 0, 512, b_f)
                    else:
                        P.op("act", lambda e: e.activation(out=f[:], in_=pf[:], func=AF.Copy),
                             reads=[b_pf], writes=[b_f])
                        store_fm(kta_d, ((mb - 2) * 128, (mb - 2) * 128 + 128), g * 512, f, 0, 512, b_f)
                        newgens.append(tm_gen(f, b_f, ka_d, (mb - 2) * 128, g * 512, 0, 512))
                else:
                    m = mb - 4
                    rw, b_rw = raw2[g % 2][m]
                    rwp, b_rwp = raw2[(g - 1) % 2][m]
                    P.op("act", lambda e: e.activation(out=rw[:, 4:516], in_=pf[:], func=AF.Copy),
                         reads=[b_pf], writes=[b_rw])
                    P.op("act", lambda e: e.activation(out=rw[:, 0:4], in_=rwp[:, 512:516], func=AF.Copy),
                         reads=[b_rwp], writes=[b_rw])
                    newgens.append(gdn_post(g, m))

            def step_pending(nsteps=1):
                for _ in range(nsteps):
                    alive = []
                    for g_ in pending:
                        try:
                            next(g_)
                            alive.append(g_)
                        except StopIteration:
                            pass
                    pending[:] = alive

            def unit_tok(g, bl):
                hTg, b_hTg = hT[g % 2]
                pt, b_pt = pTm[bl % 2]
                P.op("pe", [(lambda e, k=k: e.matmul(
                    pt[:, 0:NT], lhsT=hTg[:, k, bl * 128:(bl + 1) * 128], rhs=w1t[:, k, :],
                    start=(k == 0), stop=(k == KC - 1))) for k in range(KC)],
                    reads=[b_w1t, b_hTg], writes=[b_pt])
                v, b_v = vo[bl % 4]
                P.op("act", lambda e: e.activation(out=v[:], in_=pt[:, 0:256], func=AF.Copy),
                     reads=[b_pt], writes=[b_v])
                ob = Buf()
                outbufs.append(ob)
                n = g * 4 + bl
                P.dma("sp", lambda e: e.dma_start(out=va_d[n * 128:(n + 1) * 128, :], in_=v[:]),
                      reads=[b_v], writes=[ob])
                P.op("dve", lambda e: e.tensor_tensor(out=z4[:, bl, :], in0=pt[:, 256:268], in1=gb[:], op=ALU.add),
                     reads=[b_gb, b_pt], writes=[b_z4])

            def unit_gates(g):
                gg, b_gg = g4[g % 2]
                P.op("dve", lambda e: e.tensor_copy(out=gg[:, :, 0:2], in_=z4[:, :, 0:2]), reads=[b_z4], writes=[b_gg])
                P.op("act", lambda e: e.activation(out=e4[:, :, 0:6], in_=z4[:, :, 2:8], func=AF.Exp, scale=-1.0),
                     reads=[b_z4], writes=[b_e4])
                P.op("act", lambda e: e.activation(out=e4[:, :, 6:10], in_=z4[:, :, 8:12], func=AF.Exp),
                     reads=[b_z4], writes=[b_e4])
                P.op("act", lambda e: e.activation(out=gg[:, :, 2:4], in_=e4[:, :, 0:2], func=AF.Ln, bias=1.0),
                     reads=[b_e4], writes=[b_gg])
                P.op("dve", lambda e: e.tensor_scalar(out=gg[:, :, 2:4], in0=gg[:, :, 2:4], scalar1=-1.0,
                                                      scalar2=None, op0=ALU.mult), reads=[b_gg], writes=[b_gg])
                P.op("dve", lambda e: e.tensor_scalar(out=e4[:, :, 2:6], in0=e4[:, :, 2:6], scalar1=1.0,
                                                      scalar2=None, op0=ALU.add), reads=[b_e4], writes=[b_e4])
                P.op("dve", lambda e: e.reciprocal(out=gg[:, :, 4:8], in_=e4[:, :, 2:6]), reads=[b_e4], writes=[b_gg])
                P.op("act", lambda e: e.activation(out=gg[:, :, 8:12], in_=e4[:, :, 6:10], func=AF.Ln, bias=1.0),
                     reads=[b_e4], writes=[b_gg])
                for bl in range(4):
                    P.op("dve", lambda e, bl=bl: e.tensor_tensor(out=gg[:, bl, 8:12], in0=gg[:, bl, 8:12],
                                                                 in1=negA[:], op=ALU.mult),
                         reads=[b_gg, b_negA], writes=[b_gg])
                ob = Buf()
                outbufs.append(ob)
                P.dma("sp", lambda e: e.dma_start(
                    out=gts_d[g * 512:(g + 1) * 512, :].rearrange("(b p) c -> p b c", p=128), in_=gg[:]),
                    reads=[b_gg], writes=[ob])

            for n in range(min(NXB, 4 * ngrp)):
                load_x(n)
            for bl in range(4):
                unit_norm(bl)
            newgens = []
            for g in range(ngrp):
                units = [(lambda mb=mb: unit_feat(g, mb)) for mb in range(10)]
                units += [(lambda bl=bl: unit_tok(g, bl)) for bl in range(4)]
                units.append(lambda: unit_gates(g))
                norms = [(lambda n=n: unit_norm(n)) for n in range(4 * (g + 1), 4 * (g + 2))] if g + 1 < ngrp else []
                for i, u in enumerate(units):
                    u()
                    step_pending(1)
                    if i in (1, 4, 7, 10) and norms:
                        norms.pop(0)()
                for nn in norms:
                    nn()
                while pending:
                    step_pending(1)
                pending.extend(newgens)
                newgens = []
            for m in range(6):
                rw, b_rw = raw2[ngrp % 2][m]
                rwp, b_rwp = raw2[(ngrp - 1) % 2][m]
                P.op("pool", lambda e, rw=rw: e.memset(rw[:, 4:516], 0.0), writes=[b_rw])
                P.op("act", lambda e, rw=rw, rwp=rwp: e.activation(out=rw[:, 0:4], in_=rwp[:, 512:516], func=AF.Copy),
                     reads=[b_rwp], writes=[b_rw])
                newgens.append(gdn_post(ngrp, m))
            while pending:
                step_pending(1)
            pending.extend(newgens)
            while pending:
                step_pending(1)
            P.wait_all("sp", outbufs)
            P.emit()
    return nc


def prep_A_inputs(inp, r):
    w_in = inp["w_in"][0]
    ha, hv = r // 2, r % 2
    hb = (2 * r, 2 * r + 1)
    o_aq, o_ak, o_av = 0, 1024, 2048
    o_ai, o_af = 6144, 6152
    o_bq = 6160
    o_bk = o_bq + 2048
    o_bv = o_bq + 4096
    o_beta, o_ba = 14352, 14384
    cols_f = list(range(o_aq + ha * 256, o_aq + ha * 256 + 256)) + list(range(o_ak + ha * 256, o_ak + ha * 256 + 256))
    for base in (o_bq, o_bk, o_bv):
        for h in hb:
            cols_f += list(range(base + h * 128, base + h * 128 + 128))
    cols_t = list(range(o_av + ha * 512 + hv * 256, o_av + ha * 512 + hv * 256 + 256))
    cols_t += [o_ai + 0 * 4 + ha, o_ai + 1 * 4 + ha, o_af + 0 * 4 + ha, o_af + 1 * 4 + ha]
    cols_t += [o_beta + 0 * 16 + hb[0], o_beta + 0 * 16 + hb[1], o_beta + 1 * 16 + hb[0], o_beta + 1 * 16 + hb[1]]
    cols_t += [o_ba + 0 * 16 + hb[0], o_ba + 0 * 16 + hb[1], o_ba + 1 * 16 + hb[0], o_ba + 1 * 16 + hb[1]]
    bi, bf = inp["mlstm_b_i"][0], inp["mlstm_b_f"][0]
    dtb, Al = inp["gdn_dt_bias"][0], inp["gdn_A_log"][0]
    gbias = np.array([bi[0, ha], bi[1, ha], bf[0, ha], bf[1, ha], 0, 0, 0, 0,
                      dtb[0, hb[0]], dtb[0, hb[1]], dtb[1, hb[0]], dtb[1, hb[1]]], np.float32)
    gA = np.array([Al[0, hb[0]], Al[0, hb[1]], Al[1, hb[0]], Al[1, hb[1]]], np.float32)
    cwt = inp["gdn_conv_w"][0]
    ccols = []
    for base in (0, 2048, 4096):
        for h in hb:
            ccols += list(range(base + h * 128, base + h * 128 + 128))
    conv_w = np.ascontiguousarray(cwt[:, ccols].T)
    return {
        "x": np.ascontiguousarray(inp["x"][0]),
        "c": np.ascontiguousarray(inp["c"][0]),
        "ada_w2": np.ascontiguousarray(np.concatenate(
            [inp["ada_w"][0][:, 0:2 * D], inp["ada_w"][0][:, 2 * D + r * 1024: 2 * D + (r + 1) * 1024]], axis=1)),
        "ada_b2": np.ascontiguousarray(np.concatenate(
            [inp["ada_b"][0][0:2 * D], inp["ada_b"][0][2 * D + r * 1024: 2 * D + (r + 1) * 1024]])),
        "norm_mix_w": np.ascontiguousarray(inp["norm_mix_w"][0]),
        "w1f": np.ascontiguousarray(w_in[:, cols_f]),
        "w1t": np.ascontiguousarray(w_in[:, cols_t]),
        "gate_bias": gbias,
        "gate_A": gA,
        "conv_w": conv_w,
    }


NEGBIG = 30000.0


def FR(ap):
    return ap


def F(ap):
    return ap.bitcast(F32) if USE_F32R else ap


def build_B(nchunk=128, sh=None):
    nc = sh["nc"] if sh else bass.Bass("TRN2", target_bir_lowering=False)

    def din(name, shape, dt):
        if sh:
            return sh["scr"][name]
        return nc.dram_tensor(name, shape, dt, kind="ExternalInput").ap()
    qta_d = din("qta", [256, T], BF16)
    kta_d = din("kta", [256, T], BF16)
    ka_d = din("ka", [T, 256], BF16)
    va_d = din("va", [T, 256], BF16)
    qtb_d = din("qtb", [256, T], BF16)
    ktb_d = din("ktb", [256, T], BF16)
    kb_d = din("kb", [T, 256], BF16)
    vb_d = din("vb", [T, 256], BF16)
    gts_d = din("gts", [T, 12], F32)
    hab_d = nc.dram_tensor("hab", [T, 1024], F32, kind="ExternalOutput").ap()
    outbufs = []

    with ExitStack() as es:
        if sh:
            P = sh["P"]
        else:
            sems = [es.enter_context(nc.semaphore(f"s{i}")) for i in range(100)]
            P = Prog(nc, sems)
        C = Ctx(nc, es)
        ident_b, b_idb, idf, b_idf = make_identity(P, C)
        identf = idf[0:64, 0:64]
        idr_t, b_idr = C.sb([64, 64], F32R)
        P.op("dve", lambda e: e.tensor_copy(out=idr_t[:], in_=idf[0:64, 0:64]), reads=[b_idf], writes=[b_idr])
        identr = idr_t[:]
        ones_f, b_ones = C.sb([64, 128], F32)
        P.op("pool", lambda e: e.memset(ones_f[:], 1.0), writes=[b_ones])
        VI = [C.sb([64, 64], F32) for _ in range(2)]
        NI = [C.sb([64, 64], F32) for _ in range(2)]
        NS = [C.sb([64, 64], F32) for _ in range(2)]
        for d in range(2):
            t, b = VI[d]
            P.op("pool", lambda e, t=t: e.memset(t[:], 1.0), writes=[b])
            cm, pc = (-1, 1) if d == 0 else (1, -1)
            P.op("pool", lambda e, t=t, cm=cm, pc=pc: e.affine_select(
                out=t[:], in_=t[:], pattern=[[pc, 64]], compare_op=ALU.is_ge, fill=0.0, base=0,
                channel_multiplier=cm), reads=[b], writes=[b])
            n, bn = NI[d]
            P.op("dve", lambda e, t=t, n=n: e.tensor_scalar(out=n[:], in0=t[:], scalar1=-1.0, scalar2=NEGBIG,
                                                            op0=ALU.add, op1=ALU.mult), reads=[b], writes=[bn])
        for d in range(2):
            src, bsrc = VI[1 - d]
            n, bn = NS[d]
            P.op("dve", lambda e, src=src, n=n: e.tensor_tensor(out=n[:], in0=src[:], in1=identf, op=ALU.subtract),
                 reads=[bsrc, b_idf], writes=[bn])
            P.op("dve", lambda e, n=n: e.tensor_scalar(out=n[:], in0=n[:], scalar1=-1.0, scalar2=NEGBIG,
                                                       op0=ALU.add, op1=ALU.mult), reads=[bn], writes=[bn])
        Gs, b_Gs = C.sb([64, 128, 12], F32)
        P.dma("sp", lambda e: e.dma_start(out=Gs[:], in_=gts_d.rearrange("(c p) g -> p c g", p=64)), writes=[b_Gs])

        PB = [C.ps([128, 512], F32) for _ in range(8)]
        pbi = [0]

        def pb():
            t = PB[pbi[0] % 8]
            pbi[0] += 1
            return t

        chains = []
        for d in range(2):
            ch = {"kind": "m", "d": d}
            ch["CC"], ch["bCC"] = C.sb([64, 128], F32)
            ch["BC"], ch["bBC"] = C.sb([64, 128], F32)
            pcc, b_pcc = pb()
            P.op("pe", lambda e, d=d, pcc=pcc: e.matmul(pcc[0:64, 0:128], lhsT=VI[d][0][:], rhs=Gs[:, :, 2 + d],
                                                        start=True, stop=True),
                 reads=[VI[d][1], b_Gs], writes=[b_pcc])
            P.op("dve", lambda e, d=d, pcc=pcc, ch=ch: e.tensor_tensor(out=ch["BC"][:], in0=Gs[:, :, d],
                                                                       in1=pcc[0:64, 0:128], op=ALU.subtract),
                 reads=[b_Gs, b_pcc], writes=[ch["bBC"]])
            chains.append(ch)
        for d in range(2):
            for j in range(2):
                ch = {"kind": "g", "d": d, "j": j, "colg": 8 + d * 2 + j, "colb": 4 + d * 2 + j}
                ch["CC"], ch["bCC"] = C.sb([64, 128], F32)
                ch["NCC"], ch["bNCC"] = C.sb([64, 128], F32)
                ch["BEG"], ch["bBEG"] = C.sb([64, 128], F32)
                pcc, b_pcc = pb()
                P.op("pe", lambda e, d=d, pcc=pcc, ch=ch: e.matmul(pcc[0:64, 0:128], lhsT=VI[d][0][:],
                                                                   rhs=Gs[:, :, ch["colg"]], start=True, stop=True),
                     reads=[VI[d][1], b_Gs], writes=[b_pcc])
                P.op("act", lambda e, pcc=pcc, ch=ch: e.activation(out=ch["CC"][:], in_=pcc[0:64, 0:128], func=AF.Copy),
                     reads=[b_pcc], writes=[ch["bCC"]])
                P.op("dve", lambda e, pcc=pcc, ch=ch: e.tensor_scalar(out=ch["NCC"][:], in0=pcc[0:64, 0:128],
                                                                      scalar1=-1.0, scalar2=None, op0=ALU.mult),
                     reads=[b_pcc], writes=[ch["bNCC"]])
                P.op("act", lambda e, ch=ch: e.activation(out=ch["BEG"][:], in_=ch["CC"][:], func=AF.Exp),
                     reads=[ch["bCC"]], writes=[ch["bBEG"]])
                P.op("dve", lambda e, ch=ch: e.tensor_tensor(out=ch["BEG"][:], in0=ch["BEG"][:],
                                                             in1=Gs[:, :, ch["colb"]], op=ALU.mult),
                     reads=[ch["bBEG"], b_Gs], writes=[ch["bBEG"]])
                chains.append(ch)

        NSLOT = int(os.environ.get("NSLOT", "3"))
        LOOK = int(os.environ.get("LOOK", "2"))
        GC = 4
        NGB = int(os.environ.get("NGB", "3"))

        def W(dst, name, shape, dt):
            t, b = C.sb(shape, dt)
            dst[name] = t
            dst["b_" + name] = b

        for ch in chains:
            ch["slots"] = []
            if ch["kind"] == "m":
                W(ch, "Caug", [128, 2, 257], F32)
                W(ch, "Cb", [128, 2, 257], BF16)
                P.op("pool", lambda e, ch=ch: e.memset(ch["Caug"][:], 0.0), writes=[ch["b_Caug"]])
                P.op("pool", lambda e, ch=ch: e.memset(ch["Cb"][:], 0.0), writes=[ch["b_Cb"]])
                W(ch, "den", [64, 2], F32)
                ch["ho"] = [C.sb([64, 256], F32) for _ in range(3)]
            else:
                W(ch, "S", [128, 128], F32)
                W(ch, "Sb", [128, 128], BF16)
                P.op("pool", lambda e, ch=ch: e.memset(ch["S"][:], 0.0), writes=[ch["b_S"]])
                P.op("pool", lambda e, ch=ch: e.memset(ch["Sb"][:], 0.0), writes=[ch["b_Sb"]])
                W(ch, "vn", [64, 128], BF16)
                ch["og"] = [C.sb([64, 128], F32) for _ in range(3)]
            for s_ in range(NSLOT):
                sl = {}
                W(sl, "rc", [64, 64], F32)
                W(sl, "cum", [128, 64], F32)
                W(sl, "ecb", [128, 64], F32)
                W(sl, "E", [64, 64], F32)
                if ch["kind"] == "m":
                    W(sl, "qs", [128, 2, 64], BF16)
                    W(sl, "DT", [64, 64], F32)
                    W(sl, "PT", [64, 64], BF16)
                    W(sl, "wcol", [64, 1], F32)
                    W(sl, "kw", [64, 256], BF16)
                else:
                    W(sl, "qg", [128, 64], BF16)
                    W(sl, "Dl", [64, 64], F32)
                    W(sl, "AT", [64, 64], BF16)
                    W(sl, "E2", [64, 64], F32)
                    W(sl, "Dn", [64, 64], F32)
                    for nm in ("A0", "A1", "B0", "B1", "X"):
                        W(sl, nm, [64, 64], F32R)
                    W(sl, "Tt", [64, 64], BF16)
                    W(sl, "bv", [64, 128], BF16)
                    W(sl, "kbg", [64, 128], BF16)
                    W(sl, "kd", [64, 128], BF16)
                    W(sl, "dcol", [64, 1], F32)
                    W(sl, "u", [64, 128], F32)
                    W(sl, "wT", [128, 64], BF16)
                ch["slots"].append(sl)
            ch["nout"] = 0

        GT_ = GC * 64
        data = {}
        for d in range(2):
            for bf in range(NGB):
                dd = {}
                dd["mqT"] = C.sb([128, 2, GT_], BF16)
                dd["mkT"] = C.sb([128, 2, GT_], BF16)
                dd["mk"] = C.sb([64, GC, 256], BF16)
                dd["mv"] = C.sb([64, GC, 257], BF16)
                P.op("pool", lambda e, t=dd["mv"][0]: e.memset(t[:, :, 256:257], 1.0), writes=[dd["mv"][1]])
                for j in range(2):
                    dd["gqT%d" % j] = C.sb([128, GT_], BF16)
                    dd["gkT%d" % j] = C.sb([128, GT_], BF16)
                    dd["gk%d" % j] = C.sb([64, GC, 128], BF16)
                    dd["gv%d" % j] = C.sb([64, GC, 128], BF16)
                data[(d, bf)] = dd

        def load_group(d, g):
            dd = data[(d, g % NGB)]
            t0 = g * GT_
            ld = lambda key, fn: P.dma("sp", fn, writes=[dd[key][1]])
            ld("mqT", lambda e: e.dma_start(out=dd["mqT"][0][:],
                                            in_=qta_d[:, t0:t0 + GT_].rearrange("(h p) t -> p h t", p=128)))
            ld("mkT", lambda e: e.dma_start(out=dd["mkT"][0][:],
                                            in_=kta_d[:, t0:t0 + GT_].rearrange("(h p) t -> p h t", p=128)))
            ld("mk", lambda e: e.dma_start(out=dd["mk"][0][:],
                                           in_=ka_d[t0:t0 + GT_, :].rearrange("(c p) x -> p c x", p=64)))
            ld("mv", lambda e: e.dma_start(out=dd["mv"][0][:, :, 0:256],
                                           in_=va_d[t0:t0 + GT_, :].rearrange("(c p) x -> p c x", p=64)))
            for j in range(2):
                ld("gqT%d" % j, lambda e, j=j: e.dma_start(out=dd["gqT%d" % j][0][:],
                                                           in_=qtb_d[j * 128:(j + 1) * 128, t0:t0 + GT_]))
                ld("gkT%d" % j, lambda e, j=j: e.dma_start(out=dd["gkT%d" % j][0][:],
                                                           in_=ktb_d[j * 128:(j + 1) * 128, t0:t0 + GT_]))
                ld("gk%d" % j, lambda e, j=j: e.dma_start(
                    out=dd["gk%d" % j][0][:],
                    in_=kb_d[t0:t0 + GT_, j * 128:(j + 1) * 128].rearrange("(c p) x -> p c x", p=64)))
                ld("gv%d" % j, lambda e, j=j: e.dma_start(
                    out=dd["gv%d" % j][0][:],
                    in_=vb_d[t0:t0 + GT_, j * 128:(j + 1) * 128].rearrange("(c p) x -> p c x", p=64)))

        def cum_common(ch, sl, c, gcol):
            d = ch["d"]
            P.op("pool", lambda e: e.tensor_scalar(out=sl["rc"][:], in0=VI[d][0][:], scalar1=Gs[:, c, gcol:gcol + 1],
                                                   scalar2=None, op0=ALU.mult),
                 reads=[VI[d][1], b_Gs], writes=[sl["b_rc"]])
            pk, b_pk = pb()
            P.op("pe", lambda e: e.matmul(pk[:, 0:64], lhsT=ones_f[:], rhs=sl["rc"][:], start=True, stop=True),
                 reads=[b_ones, sl["b_rc"]], writes=[b_pk])
            P.op("act", lambda e: e.activation(out=sl["cum"][:], in_=pk[:, 0:64], func=AF.Copy),
                 reads=[b_pk], writes=[sl["b_cum"]])
            P.op("act", lambda e: e.activation(out=sl["ecb"][:], in_=sl["cum"][:], func=AF.Exp),
                 reads=[sl["b_cum"]], writes=[sl["b_ecb"]])
            P.op("dve", lambda e: e.tensor_tensor(out=sl["E"][:], in0=sl["cum"][0:64, :], in1=NI[d][0][:], op=ALU.add),
                 reads=[sl["b_cum"], NI[d][1]], writes=[sl["b_E"]])

        def dtiles(ch, c):
            d = ch["d"]
            dd = data[(d, (c // GC) % NGB)]
            lc = c % GC
            return dd, lc, slice(lc * 64, lc * 64 + 64), (63 if d == 0 else 0)

        def mlstm_pre(ch, c, sl):
            d = ch["d"]
            dd, lc, ts, last = dtiles(ch, c)
            qT, b_qT = dd["mqT"]
            kT, b_kT = dd["mkT"]
            kk, b_kk = dd["mk"]
            cum_common(ch, sl, c, 2 + d)
            yield
            for h in range(2):
                P.op("dve", lambda e, h=h: e.tensor_tensor(out=sl["qs"][:, h, :], in0=qT[:, h, ts], in1=sl["ecb"][:],
                                                            op=ALU.mult),
                     reads=[b_qT, sl["b_ecb"]], writes=[sl["b_qs"]])
            P.op("act", lambda e: e.activation(out=sl["DT"][:], in_=sl["E"][:], func=AF.Exp, bias=ch["BC"][:, c:c + 1]),
                 reads=[sl["b_E"], ch["bBC"]], writes=[sl["b_DT"]])
            P.op("act", lambda e: e.activation(out=sl["wcol"][:], in_=sl["cum"][0:64, last:last + 1], func=AF.Exp,
                                               bias=ch["BC"][:, c:c + 1]),
                 reads=[sl["b_cum"], ch["bBC"]], writes=[sl["b_wcol"]])
            yield
            pst, b_pst = pb()
            P.op("pe", [(lambda e, h=h: e.matmul(pst[0:64, 0:64], lhsT=kT[:, h, ts], rhs=qT[:, h, ts],
                                                 start=(h == 0), stop=(h == 1))) for h in range(2)],
                 reads=[b_kT, b_qT], writes=[b_pst])
            P.op("act", lambda e: e.activation(out=sl["kw"][:], in_=kk[:, lc, :], func=AF.Copy, scale=sl["wcol"][:, 0:1]),
                 reads=[b_kk, sl["b_wcol"]], writes=[sl["b_kw"]])
            P.op("dve", lambda e: e.tensor_tensor(out=sl["PT"][:], in0=pst[0:64, 0:64], in1=sl["DT"][:], op=ALU.mult),
                 reads=[b_pst, sl["b_DT"]], writes=[sl["b_PT"]])
            yield

        def mlstm_seq(ch, c, sl):
            d = ch["d"]
            dd, lc, ts, last = dtiles(ch, c)
            vv, b_vv = dd["mv"]
            pn, b_pn = pb()
            P.op("pe", [lambda e: e.matmul(pn[0:64, 0:257], lhsT=sl["qs"][:, 0, :], rhs=ch["Cb"][:, 0, :],
                                           start=True, stop=False),
                        lambda e: e.matmul(pn[0:64, 0:257], lhsT=sl["qs"][:, 1, :], rhs=ch["Cb"][:, 1, :],
                                           start=False, stop=False),
                        lambda e: e.matmul(pn[0:64, 0:257], lhsT=sl["PT"][:], rhs=vv[:, lc, :],
                                           start=False, stop=True)],
                 reads=[sl["b_qs"], ch["b_Cb"], sl["b_PT"], b_vv], writes=[b_pn])
            pu = [pb(), pb()]
            for h in range(2):
                P.op("pe", lambda e, h=h: e.matmul(pu[h][0][:, 0:257], lhsT=sl["kw"][:, h * 128:(h + 1) * 128],
                                                   rhs=vv[:, lc, :], start=True, stop=True),
                     reads=[sl["b_kw"], b_vv], writes=[pu[h][1]])
            for h in range(2):
                P.op("dve", lambda e, h=h: e.scalar_tensor_tensor(
                    out=ch["Caug"][:, h, :], in0=ch["Caug"][:, h, :], scalar=sl["ecb"][:, last:last + 1],
                    in1=pu[h][0][:, 0:257], op0=ALU.mult, op1=ALU.add),
                    reads=[sl["b_ecb"], pu[h][1]], writes=[ch["b_Caug"]])
            P.op("act", lambda e: e.activation(out=ch["den"][:, 0:1], in_=pn[0:64, 256:257], func=AF.Abs),
                 reads=[b_pn], writes=[ch["b_den"]])
            P.op("act", lambda e: e.activation(out=ch["Cb"][:], in_=ch["Caug"][:], func=AF.Copy),
                 reads=[ch["b_Caug"]], writes=[ch["b_Cb"]])
            P.op("dve", lambda e: e.tensor_scalar(out=ch["den"][:, 0:1], in0=ch["den"][:, 0:1], scalar1=1.0,
                                                  scalar2=None, op0=ALU.max), reads=[ch["b_den"]], writes=[ch["b_den"]])
            P.op("dve", lambda e: e.reciprocal(out=ch["den"][:, 1:2], in_=ch["den"][:, 0:1]),
                 reads=[ch["b_den"]], writes=[ch["b_den"]])
            ho, b_ho = ch["ho"][ch["nout"] % 3]
            ch["nout"] += 1
            P.op("act", lambda e: e.activation(out=ho[:], in_=pn[0:64, 0:256], func=AF.Copy, scale=ch["den"][:, 1:2]),
                 reads=[b_pn, ch["b_den"]], writes=[b_ho])
            ob = Buf()
            outbufs.append(ob)
            P.dma("sp", lambda e: e.dma_start(out=hab_d[c * 64:(c + 1) * 64, d * 256:(d + 1) * 256], in_=ho[:]),
                  reads=[b_ho], writes=[ob])
            yield

        def gdn_pre(ch, c, sl):
            d, j = ch["d"], ch["j"]
            dd, lc, ts, last = dtiles(ch, c)
            qT, b_qT = dd["gqT%d" % j]
            kT, b_kT = dd["gkT%d" % j]
            kk, b_kk = dd["gk%d" % j]
            vv, b_vv = dd["gv%d" % j]
            beta = Gs[:, c, ch["colb"]:ch["colb"] + 1]
            cum_common(ch, sl, c, ch["colg"])
            P.op("dve", lambda e: e.tensor_scalar(out=sl["bv"][:], in0=vv[:, lc, :], scalar1=beta, scalar2=None,
                                                  op0=ALU.mult), reads=[b_vv, b_Gs], writes=[sl["b_bv"]])
            P.op("act", lambda e: e.activation(out=sl["kbg"][:], in_=kk[:, lc, :], func=AF.Copy, scale=ch["BEG"][:, c:c + 1]),
                 reads=[b_kk, ch["bBEG"]], writes=[sl["b_kbg"]])
            yield
            P.op("dve", lambda e: e.tensor_tensor(out=sl["qg"][:], in0=qT[:, ts], in1=sl["ecb"][:], op=ALU.mult),
                 reads=[b_qT, sl["b_ecb"]], writes=[sl["b_qg"]])
            P.op("act", lambda e: e.activation(out=sl["Dl"][:], in_=sl["E"][:], func=AF.Exp, bias=ch["NCC"][:, c:c + 1]),
                 reads=[sl["b_E"], ch["bNCC"]], writes=[sl["b_Dl"]])
            P.op("dve", lambda e: e.scalar_tensor_tensor(out=sl["E2"][:], in0=sl["cum"][0:64, :], scalar=-1.0,
                                                         in1=NS[d][0][:], op0=ALU.mult, op1=ALU.add),
                 reads=[sl["b_cum"], NS[d][1]], writes=[sl["b_E2"]])
            P.op("act", lambda e: e.activation(out=sl["Dn"][:], in_=sl["E2"][:], func=AF.Exp, bias=ch["CC"][:, c:c + 1]),
                 reads=[sl["b_E2"], ch["bCC"]], writes=[sl["b_Dn"]])
            P.op("act", lambda e: e.activation(out=sl["dcol"][:], in_=sl["cum"][0:64, last:last + 1], func=AF.Exp,
                                               bias=ch["NCC"][:, c:c + 1]),
                 reads=[sl["b_cum"], ch["bNCC"]], writes=[sl["b_dcol"]])
            yield
            pkq, b_pkq = pb()
            P.op("pe", [lambda e: e.matmul(pkq[0:64, 0:64], lhsT=kT[:, ts], rhs=kT[:, ts], start=True, stop=True),
                        lambda e: e.matmul(pkq[0:64, 64:128], lhsT=kT[:, ts], rhs=qT[:, ts], start=True, stop=True)],
                 reads=[b_kT, b_qT], writes=[b_pkq])
            P.op("dve", lambda e: e.tensor_scalar(out=sl["kd"][:], in0=kk[:, lc, :], scalar1=sl["dcol"][:, 0:1],
                                                  scalar2=None, op0=ALU.mult),
                 reads=[b_kk, sl["b_dcol"]], writes=[sl["b_kd"]])
            P.op("dve", lambda e: e.tensor_tensor(out=sl["AT"][:], in0=pkq[0:64, 64:128], in1=sl["Dl"][:], op=ALU.mult),
                 reads=[b_pkq, sl["b_Dl"]], writes=[sl["b_AT"]])
            P.op("dve", lambda e: e.scalar_tensor_tensor(out=sl["A0"][:], in0=pkq[0:64, 0:64], scalar=beta,
                                                         in1=sl["Dn"][:], op0=ALU.mult, op1=ALU.mult),
                 reads=[b_pkq, b_Gs, sl["b_Dn"]], writes=[sl["b_A0"]])
            yield
            pB, b_pB = pb()
            P.op("pe", lambda e: e.matmul(pB[0:64, 0:64], lhsT=sl["A0"][:], rhs=identr, start=True, stop=True),
                 reads=[sl["b_A0"], b_idr], writes=[b_pB])
            P.op("act", lambda e: e.activation(out=sl["B0"][:], in_=pB[0:64, 0:64], func=AF.Copy),
                 reads=[b_pB], writes=[sl["b_B0"]])
            P.op("dve", lambda e: e.tensor_tensor(out=sl["X"][:], in0=identf, in1=pB[0:64, 0:64], op=ALU.subtract),
                 reads=[b_idf, b_pB], writes=[sl["b_X"]])
            yield
            cur = 0
            for lvl in range(5):
                A, bA = sl["A%d" % cur], sl["b_A%d" % cur]
                B, bB = sl["B%d" % cur], sl["b_B%d" % cur]
                An, bAn = sl["A%d" % (1 - cur)], sl["b_A%d" % (1 - cur)]
                Bn, bBn = sl["B%d" % (1 - cur)], sl["b_B%d" % (1 - cur)]
                pA, b_pA = pb()
                P.op("pe", lambda e, A=A, B=B, pA=pA: e.matmul(pA[0:64, 0:64], lhsT=FR(B[:]), rhs=FR(A[:]), start=True, stop=True),
                     reads=[bA, bB], writes=[b_pA])
                if lvl < 4:
                    pBn, b_pBn = pb()
                    P.op("pe", lambda e, A=A, B=B, pBn=pBn: e.matmul(pBn[0:64, 0:64], lhsT=FR(A[:]), rhs=FR(B[:]),
                                                                     start=True, stop=True),
                         reads=[bA, bB], writes=[b_pBn])
                P.op("act", lambda e, An=An, pA=pA: e.activation(out=An[:], in_=pA[0:64, 0:64], func=AF.Copy),
                     reads=[b_pA], writes=[bAn])
                if lvl < 4:
                    P.op("dve", lambda e, Bn=Bn, pBn=pBn: e.tensor_copy(out=Bn[:], in_=pBn[0:64, 0:64]),
                         reads=[b_pBn], writes=[bBn])
                yield
                pX, b_pX = pb()
                P.op("pe", lambda e, An=An, pX=pX: e.matmul(pX[0:64, 0:64], lhsT=FR(An[:]), rhs=FR(sl["X"][:]),
                                                            start=True, stop=True),
                     reads=[bAn, sl["b_X"]], writes=[b_pX])
                if lvl < 4:
                    P.op("dve", lambda e, pX=pX: e.tensor_tensor(out=sl["X"][:], in0=pX[0:64, 0:64], in1=F(sl["X"][:]),
                                                                 op=ALU.add),
                         reads=[b_pX, sl["b_X"]], writes=[sl["b_X"]])
                else:
                    P.op("dve", lambda e, pX=pX: e.tensor_tensor(out=sl["Tt"][:], in0=pX[0:64, 0:64], in1=F(sl["X"][:]),
                                                                 op=ALU.add),
                         reads=[b_pX, sl["b_X"]], writes=[sl["b_Tt"]])
                cur = 1 - cur
                yield
            pu, b_pu = pb()
            P.op("pe", lambda e: e.matmul(pu[0:64, 0:128], lhsT=sl["Tt"][:], rhs=sl["bv"][:], start=True, stop=True),
                 reads=[sl["b_Tt"], sl["b_bv"]], writes=[b_pu])
            pw, b_pw = pb()
            P.op("pe", lambda e: e.matmul(pw[:, 0:64], lhsT=sl["kbg"][:], rhs=sl["Tt"][:], start=True, stop=True),
                 reads=[sl["b_kbg"], sl["b_Tt"]], writes=[b_pw])
            P.op("act", lambda e: e.activation(out=sl["u"][:], in_=pu[0:64, 0:128], func=AF.Copy),
                 reads=[b_pu], writes=[sl["b_u"]])
            P.op("dve", lambda e: e.tensor_copy(out=sl["wT"][:], in_=pw[:, 0:64]), reads=[b_pw], writes=[sl["b_wT"]])
            yield

        def gdn_seq(ch, c, sl):
            d, j = ch["d"], ch["j"]
            dd, lc, ts, last = dtiles(ch, c)
            pws, b_pws = pb()
            P.op("pe", lambda e: e.matmul(pws[0:64, 0:128], lhsT=sl["wT"][:], rhs=ch["Sb"][:], start=True, stop=True),
                 reads=[sl["b_wT"], ch["b_Sb"]], writes=[b_pws])
            P.op("dve", lambda e: e.tensor_tensor(out=ch["vn"][:], in0=sl["u"][:], in1=pws[0:64, 0:128], op=ALU.subtract),
                 reads=[sl["b_u"], b_pws], writes=[ch["b_vn"]])
            yield
            pup, b_pup = pb()
            P.op("pe", lambda e: e.matmul(pup[:, 0:128], lhsT=sl["kd"][:], rhs=ch["vn"][:], start=True, stop=True),
                 reads=[sl["b_kd"], ch["b_vn"]], writes=[b_pup])
            po, b_po = pb()
            P.op("pe", [lambda e: e.matmul(po[0:64, 0:128], lhsT=sl["qg"][:], rhs=ch["Sb"][:], start=True, stop=False),
                        lambda e: e.matmul(po[0:64, 0:128], lhsT=sl["AT"][:], rhs=ch["vn"][:], start=False, stop=True)],
                 reads=[sl["b_qg"], ch["b_Sb"], sl["b_AT"], ch["b_vn"]], writes=[b_po])
            P.op("dve", lambda e: e.scalar_tensor_tensor(out=ch["S"][:], in0=ch["S"][:], scalar=sl["ecb"][:, last:last + 1],
                                                         in1=pup[:, 0:128], op0=ALU.mult, op1=ALU.add),
                 reads=[sl["b_ecb"], b_pup], writes=[ch["b_S"]])
            og, b_og = ch["og"][ch["nout"] % 3]
            ch["nout"] += 1
            P.op("act", lambda e: e.activation(out=og[:], in_=po[0:64, 0:128], func=AF.Copy), reads=[b_po], writes=[b_og])
            P.op("act", lambda e: e.activation(out=ch["Sb"][:], in_=ch["S"][:], func=AF.Copy),
                 reads=[ch["b_S"]], writes=[ch["b_Sb"]])
            ob = Buf()
            outbufs.append(ob)
            co = 512 + d * 256 + j * 128
            P.dma("sp", lambda e: e.dma_start(out=hab_d[c * 64:(c + 1) * 64, co:co + 128], in_=og[:]),
                  reads=[b_og], writes=[ob])
            yield

        ngr = (nchunk + GC - 1) // GC
        NG_ALL = 128 // GC
        loaded = set()

        def ensure_group(gi):
            if gi < ngr and gi not in loaded:
                loaded.add(gi)
                load_group(0, gi)
                load_group(1, NG_ALL - 1 - gi)

        def chunk_of(ch, i):
            return i if ch["d"] == 0 else 127 - i

        def run_round_robin(gens):
            alive = list(gens)
            while alive:
                nxt = []
                for g_ in alive:
                    try:
                        next(g_)
                        nxt.append(g_)
                    except StopIteration:
                        pass
                alive = nxt

        def pre_gens(i):
            ensure_group(i // GC)
            ensure_group(i // GC + 1)
            out = []
            for ch in chains:
                c = chunk_of(ch, i)
                sl = ch["slots"][i % NSLOT]
                out.append(mlstm_pre(ch, c, sl) if ch["kind"] == "m" else gdn_pre(ch, c, sl))
            return out

        def seq_gens(i):
            out = []
            for ch in chains:
                c = chunk_of(ch, i)
                sl = ch["slots"][i % NSLOT]
                out.append(mlstm_seq(ch, c, sl) if ch["kind"] == "m" else gdn_seq(ch, c, sl))
            return out

        for i in range(min(LOOK, nchunk)):
            run_round_robin(pre_gens(i))
        for i in range(nchunk):
            gens = seq_gens(i)
            if i + LOOK < nchunk:
                gens = gens + pre_gens(i + LOOK)
            run_round_robin(gens)
        P.wait_all("sp", outbufs)
        P.emit()
    return nc


TOK = T // NCORE
NB = TOK // 128
D_FF = 5632
NFF = D_FF // 128


def build_C():
    nc = bass.Bass("TRN2", target_bir_lowering=False)
    din = lambda name, shape: nc.dram_tensor(name, shape, F32, kind="ExternalInput").ap()
    x_d = din("x", [TOK, D])
    modin_d = din("modfm", [128, 96])
    nmix_d = din("norm_mix_w", [D])
    nffn_d = din("norm_ffn_w", [D])
    nfin_d = din("norm_final_w", [D])
    nwa_d = din("mlstm_norm_w", [D])
    nwb_d = din("gdn_norm_w", [D])
    w3_d = din("w3", [D, 4 * D])
    wa_d = din("w_branch_a", [D, D])
    wb_d = din("w_branch_b", [D, D])
    wo_d = din("w_out", [D, D])
    wgu_d = din("w_gate_up", [D, 2 * D_FF])
    wd_d = din("w_down", [D_FF, D])
    hsrc = [din(nm, [TOK, D]) for nm in ("haf", "hab", "obf", "obb")]
    out_d = nc.dram_tensor("out", [TOK, D], F32, kind="ExternalOutput").ap()
    x1_d = nc.dram_tensor("x1_scr", [TOK, D], F32, kind="Internal").ap()
    x2_d = nc.dram_tensor("x2_scr", [TOK, D], F32, kind="Internal").ap()
    b_x1d = [Buf() for _ in range(NB)]
    b_x2d = [[Buf() for _ in range(8)] for _ in range(NB)]
    outbufs = []

    with ExitStack() as es0:
        sems = [es0.enter_context(nc.semaphore(f"s{i}")) for i in range(100)]
        P = Prog(nc, sems)
        C0 = Ctx(nc, es0)
        mod, b_mod = C0.sb([128, 96], F32, "mod")
        scm, b_scm = C0.sb([128, KC], F32, "scm")
        scf, b_scf = C0.sb([128, KC], F32, "scf")
        gm_row, b_gm = C0.sb([128, D], F32, "gm_row")
        gf_row, b_gf = C0.sb([128, D], F32, "gf_row")
        ident, b_id, idf, b_idf = make_identity(P, C0)
        with ExitStack() as es1:
            C = Ctx(nc, es1)
            nw, b_nw = C.sb([128, KC], F32)
            nf, b_nf = C.sb([128, KC], F32)
            P.dma("sp", lambda e: e.dma_start(out=nw[:], in_=nmix_d.rearrange("(k p) -> p k", p=128),
                                              allow_slow_non_contiguous=True), writes=[b_nw])
            P.dma("sp", lambda e: e.dma_start(out=nf[:], in_=nffn_d.rearrange("(k p) -> p k", p=128),
                                              allow_slow_non_contiguous=True), writes=[b_nf])
            P.dma("sp", lambda e: e.dma_start(out=mod[:], in_=modin_d), writes=[b_mod])
            P.op("dve", lambda e: e.scalar_tensor_tensor(out=scm[:], in0=mod[:, 16:32], scalar=1.0, in1=nw[:],
                                                         op0=ALU.add, op1=ALU.mult), reads=[b_mod, b_nw], writes=[b_scm])
            P.op("dve", lambda e: e.scalar_tensor_tensor(out=scf[:], in0=mod[:, 64:80], scalar=1.0, in1=nf[:],
                                                         op0=ALU.add, op1=ALU.mult), reads=[b_mod, b_nf], writes=[b_scf])
            ones_f, b_ones = C.sb([128, 128], F32)
            P.op("pool", lambda e: e.memset(ones_f[:], 1.0), writes=[b_ones])
            dg, b_dg = C.sb([128, 128], F32)
            pg, b_pg = C.ps([128, 128], F32)
            for (row, b_row, off) in ((gm_row, b_gm, 32), (gf_row, b_gf, 80)):
                for k in range(KC):
                    P.op("dve", lambda e, k=k, off=off: e.tensor_scalar(out=dg[:], in0=idf[:], scalar1=mod[:, off + k:off + k + 1],
                                                                        scalar2=None, op0=ALU.mult),
                         reads=[b_idf, b_mod], writes=[b_dg])
                    P.op("pe", lambda e: e.matmul(pg[:], lhsT=ones_f[:], rhs=dg[:], start=True, stop=True),
                         reads=[b_ones, b_dg], writes=[b_pg])
                    P.op("act", lambda e, k=k, row=row: e.activation(out=row[:, k * 128:(k + 1) * 128], in_=pg[:], func=AF.Copy),
                         reads=[b_pg], writes=[b_row])
            P.emit()

        def gemm_tok(C, specs, ncb, cbw, epilogue, psums):
            wbufs = []
            for (aT, b_aT, wfn, kch) in specs:
                wbufs.append([C.sb([128, kch, cbw], BF16) for _ in range(2)])
            for cb in range(ncb):
                for si, (aT, b_aT, wfn, kch) in enumerate(specs):
                    wt, b_wt = wbufs[si][cb % 2]
                    src = wfn(cb)
                    P.dma("pool", lambda e, wt=wt, src=src: e.dma_start(
                        out=wt[:], in_=src.rearrange("(k p) n -> p k n", p=128)), writes=[b_wt])
                for b in range(NB):
                    outs = []
                    for si, (aT, b_aT, wfn, kch) in enumerate(specs):
                        wt, b_wt = wbufs[si][cb % 2]
                        pp, b_pp = psums[si][(cb * NB + b) % 2]
                        P.op("pe", [(lambda e, k=k, aT=aT, wt=wt, pp=pp, b=b, kch=kch: e.matmul(
                            pp[:, 0:cbw], lhsT=aT[:, k, b * 128:(b + 1) * 128], rhs=wt[:, k, :],
                            start=(k == 0), stop=(k == kch - 1))) for k in range(kch)],
                            reads=[b_aT, b_wt], writes=[b_pp])
                        outs.append((pp, b_pp))
                    epilogue(b, cb, outs)

        def transpose_blocks(src, b_src, dst, b_dst, pT, b_pT, copy_eng="act"):
            for b in range(NB):
                P.op("pe", [(lambda e, k=k, b=b: e.transpose(out=pT[:, k, :], in_=src[:, b, k * 128:(k + 1) * 128],
                                                             identity=ident[:])) for k in range(KC)],
                     reads=[b_src, b_id], writes=[b_pT])
                if copy_eng == "act":
                    P.op("act", lambda e, b=b: e.activation(out=dst[:, :, b * 128:(b + 1) * 128], in_=pT[:], func=AF.Copy),
                         reads=[b_pT], writes=[b_dst])
                else:
                    P.op("dve", lambda e, b=b: e.tensor_copy(out=dst[:, :, b * 128:(b + 1) * 128], in_=pT[:]),
                         reads=[b_pT], writes=[b_dst])

        def norm_to_T(C, src_fn, sc_t, b_sc, sh_ap_fn, b_sh, dstT, b_dstT, pT, b_pT):
            junk, b_junk = C.sb([128, D], BF16)
            st, b_st = C.sb([128, NB, 3], F32)
            xs2 = [C.sb([128, D], BF16) for _ in range(2)]
            pre = {0: src_fn(0)}
            for b in range(NB):
                if b + 1 < NB:
                    pre[b + 1] = src_fn(b + 1)
                xt, b_xt = pre.pop(b)
                xs, b_xs = xs2[b % 2]
                P.op("act", lambda e, b=b, xt=xt: e.activation(out=junk[:], in_=xt, func=AF.Square, accum_out=st[:, b, 0:1]),
                     reads=[b_xt], writes=[b_junk, b_st])
                P.op("act", lambda e, b=b: e.activation(out=st[:, b, 1:2], in_=st[:, b, 0:1], func=AF.Sqrt, scale=1.0 / D,
                                                        bias=EPS), reads=[b_st], writes=[b_st])
                P.op("dve", lambda e, b=b: e.reciprocal(out=st[:, b, 2:3], in_=st[:, b, 1:2]), reads=[b_st], writes=[b_st])
                P.op("dve", lambda e, b=b, xt=xt, xs=xs: e.tensor_scalar(out=xs[:], in0=xt, scalar1=st[:, b, 2:3], scalar2=None,
                                                                         op0=ALU.mult), reads=[b_st, b_xt], writes=[b_xs])
                P.op("pe", [(lambda e, k=k, xs=xs: e.transpose(out=pT[:, k, :], in_=xs[:, k * 128:(k + 1) * 128],
                                                               identity=ident[:])) for k in range(KC)],
                     reads=[b_xs, b_id], writes=[b_pT])
                P.op("act", [(lambda e, k=k, b=b: e.activation(out=dstT[:, k, b * 128:(b + 1) * 128], in_=pT[:, k, :],
                                                               func=AF.Identity, scale=sc_t[:, k:k + 1], bias=sh_ap_fn(k)))
                             for k in range(KC)], reads=[b_pT, b_sc, b_sh], writes=[b_dstT])

        CBW = 256
        NCB = D // CBW
        with ExitStack() as es2:
            C = Ctx(nc, es2)
            pT, b_pT = C.ps([128, KC, 128], BF16)
            psA = [C.ps([128, 512], F32) for _ in range(2)]
            psB = [C.ps([128, 512], F32) for _ in range(2)]
            GT, b_GT = C.sb([128, KC, TOK], BF16, "GT")
            with ExitStack() as es2m:
                Cm = Ctx(nc, es2m)
                hT, b_hT = Cm.sb([128, KC, TOK], BF16, "hT")
                Gt, b_Gt = Cm.sb([128, NB, D], BF16, "Gt")
                Mt, b_Mt = Cm.sb([128, NB, D], BF16, "Mt")
                nrow = [Cm.sb([128, D], F32) for _ in range(2)]
                P.dma("sp", lambda e: e.dma_start(out=nrow[0][0][:], in_=nwa_d.partition_broadcast(128)), writes=[nrow[0][1]])
                P.dma("sp", lambda e: e.dma_start(out=nrow[1][0][:], in_=nwb_d.partition_broadcast(128)), writes=[nrow[1][1]])
                with ExitStack() as es2a:
                    Ca = Ctx(nc, es2a)
                    xb = [Ca.sb([128, D], F32) for _ in range(2)]

                    def src_x(b):
                        xt, b_xt = xb[b % 2]
                        P.dma("sp", lambda e: e.dma_start(out=xt[:], in_=x_d[b * 128:(b + 1) * 128, :]), writes=[b_xt])
                        return xt[:], b_xt
                    norm_to_T(Ca, src_x, scm, b_scm, lambda k: mod[:, k:k + 1], b_mod, hT, b_hT, pT, b_pT)
                    P.emit()
                for br in range(2):
                    H = 4 if br == 0 else 16
                    dv = D // H
                    with ExitStack() as es2h:
                        Ch = Ctx(nc, es2h)
                        hl = [Ch.sb([128, D], F32) for _ in range(2)]
                        hr = [Ch.sb([128, D], F32) for _ in range(2)]
                        jk, b_jk = Ch.sb([128, 512], BF16)
                        ssn, b_ssn = Ch.sb([128, 16, 2], F32)
                        def load_h(b, br=br, hl=hl, hr=hr):
                            a, b_a = hl[b % 2]
                            c2, b_c2 = hr[b % 2]
                            P.dma("sp", lambda e: e.dma_start(out=a[:], in_=hsrc[2 * br][b * 128:(b + 1) * 128, :]),
                                  writes=[b_a])
                            P.dma("sp", lambda e: e.dma_start(out=c2[:], in_=hsrc[2 * br + 1][b * 128:(b + 1) * 128, :]),
                                  writes=[b_c2])
                        load_h(0)
                        for b in range(NB):
                            a, b_a = hl[b % 2]
                            c2, b_c2 = hr[b % 2]
                            if b + 1 < NB:
                                load_h(b + 1)
                            P.op("dve", lambda e, a=a, c2=c2: e.tensor_tensor(out=a[:], in0=a[:], in1=c2[:], op=ALU.add),
                                 reads=[b_a, b_c2], writes=[b_a])
                            P.op("act", [(lambda e, a=a, h=h, dv=dv: e.activation(out=jk[:, 0:dv], in_=a[:, h * dv:(h + 1) * dv],
                                                                                   func=AF.Square, accum_out=ssn[:, h, 0:1]))
                                         for h in range(H)], reads=[b_a], writes=[b_jk, b_ssn])
                            P.op("act", lambda e, H=H, dv=dv: e.activation(out=ssn[:, 0:H, 1:2], in_=ssn[:, 0:H, 0:1], func=AF.Ln,
                                                                           scale=1.0 / dv, bias=EPS), reads=[b_ssn], writes=[b_ssn])
                            P.op("act", lambda e, H=H: e.activation(out=ssn[:, 0:H, 0:1], in_=ssn[:, 0:H, 1:2], func=AF.Exp, scale=-0.5),
                                 reads=[b_ssn], writes=[b_ssn])
                            P.op("act", [(lambda e, a=a, h=h, dv=dv: e.activation(out=a[:, h * dv:(h + 1) * dv],
                                                                                   in_=a[:, h * dv:(h + 1) * dv], func=AF.Copy,
                                                                                   scale=ssn[:, h, 0:1])) for h in range(H)],
                                 reads=[b_ssn, b_a], writes=[b_a])
                            P.op("dve", lambda e, a=a, b=b, br=br: e.tensor_tensor(out=Gt[:, b, :], in0=a[:], in1=nrow[br][0][:],
                                                                                   op=ALU.mult),
                                 reads=[b_a, nrow[br][1]], writes=[b_Gt])
                        P.emit()
                    with ExitStack() as esg:
                        Cg = Ctx(nc, esg)
                        sg = [Cg.sb([128, CBW], F32) for _ in range(2)]

                        def ep_gate(b, cb, outs, br=br, sg=sg):
                            pp, b_pp = outs[0]
                            s_, b_s = sg[(cb * NB + b) % 2]
                            P.op("act", lambda e: e.activation(out=s_[:], in_=pp[:, 0:CBW],
                                                               func=(AF.Sigmoid if br == 0 else AF.Silu)),
                                 reads=[b_pp], writes=[b_s])
                            P.op("dve", lambda e: e.tensor_tensor(out=Gt[:, b, cb * CBW:(cb + 1) * CBW],
                                                                  in0=Gt[:, b, cb * CBW:(cb + 1) * CBW], in1=s_[:], op=ALU.mult),
                                 reads=[b_s, b_Gt], writes=[b_Gt])
                        gemm_tok(Cg, [(hT, b_hT, (lambda cb, br=br: w3_d[:, br * D + cb * CBW: br * D + (cb + 1) * CBW]), KC)],
                                 NCB, CBW, ep_gate, [psA])
                        P.emit()
                    transpose_blocks(Gt, b_Gt, GT, b_GT, pT, b_pT)
                    with ExitStack() as esg:
                        Cg = Ctx(nc, esg)
                        sg = [Cg.sb([128, CBW], F32) for _ in range(2)]

                        def ep_merge(b, cb, outs, br=br, sg=sg):
                            (py, b_py), (pgm, b_pgm) = outs
                            s_, b_s = sg[(cb * NB + b) % 2]
                            P.op("act", lambda e: e.activation(out=s_[:], in_=pgm[:, 0:CBW], func=AF.Sigmoid),
                                 reads=[b_pgm], writes=[b_s])
                            if br == 0:
                                P.op("dve", lambda e: e.tensor_tensor(out=Mt[:, b, cb * CBW:(cb + 1) * CBW], in0=py[:, 0:CBW],
                                                                      in1=s_[:], op=ALU.mult), reads=[b_py, b_s], writes=[b_Mt])
                            else:
                                P.op("dve", lambda e: e.tensor_tensor(out=s_[:], in0=py[:, 0:CBW], in1=s_[:], op=ALU.mult),
                                     reads=[b_py, b_s], writes=[b_s])
                                P.op("dve", lambda e: e.tensor_tensor(out=Mt[:, b, cb * CBW:(cb + 1) * CBW],
                                                                      in0=Mt[:, b, cb * CBW:(cb + 1) * CBW], in1=s_[:], op=ALU.add),
                                     reads=[b_s, b_Mt], writes=[b_Mt])
                        wbr = wa_d if br == 0 else wb_d
                        gemm_tok(Cg, [(GT, b_GT, (lambda cb, wbr=wbr: wbr[:, cb * CBW:(cb + 1) * CBW]), KC),
                                      (hT, b_hT, (lambda cb, br=br: w3_d[:, (2 + br) * D + cb * CBW:(2 + br) * D + (cb + 1) * CBW]), KC)],
                                 NCB, CBW, ep_merge, [psA, psB])
                        P.emit()
                transpose_blocks(Mt, b_Mt, GT, b_GT, pT, b_pT)
                P.emit()
            with ExitStack() as es2c:
                Cc = Ctx(nc, es2c)
                xres, b_xres = Cc.sb([128, NB, D], F32)
                P.dma("sp", lambda e: e.dma_start(out=xres[:], in_=x_d.rearrange("(b p) n -> p b n", p=128)), writes=[b_xres])
                OW = 512
                tmp = [Cc.sb([128, OW], F32) for _ in range(2)]

                def ep_out(b, cb, outs):
                    pp, b_pp = outs[0]
                    t_, b_t = tmp[(cb * NB + b) % 2]
                    P.op("dve", lambda e: e.tensor_tensor(out=t_[:], in0=pp[:, 0:OW], in1=gm_row[:, cb * OW:(cb + 1) * OW],
                                                          op=ALU.mult), reads=[b_pp, b_gm], writes=[b_t])
                    P.op("dve", lambda e: e.tensor_tensor(out=xres[:, b, cb * OW:(cb + 1) * OW],
                                                           in0=xres[:, b, cb * OW:(cb + 1) * OW], in1=t_[:], op=ALU.add),
                         reads=[b_t, b_xres], writes=[b_xres])
                gemm_tok(Cc, [(GT, b_GT, (lambda cb: wo_d[:, cb * OW:(cb + 1) * OW]), KC)], D // OW, OW, ep_out, [psA])
                for b in range(NB):
                    P.dma("sp", lambda e, b=b: e.dma_start(out=x1_d[b * 128:(b + 1) * 128, :], in_=xres[:, b, :]),
                          reads=[b_xres], writes=[b_x1d[b]])
                P.wait_all("sp", b_x1d)
                P.emit()
        with ExitStack() as es4:
            C = Ctx(nc, es4)
            actT, b_actT = C.sb([128, NFF, TOK], BF16, "actT")
            with ExitStack() as es4h:
                Chh = Ctx(nc, es4h)
                hf2, b_hf2 = Chh.sb([128, KC, TOK], BF16, "hf2")
                pT, b_pT = Chh.ps([128, KC, 128], BF16)
                pg_ = [Chh.ps([128, 512], F32) for _ in range(2)]
                pu_ = [Chh.ps([128, 512], F32) for _ in range(2)]
                with ExitStack() as es4a:
                    Ca = Ctx(nc, es4a)
                    xb = [Ca.sb([128, D], F32) for _ in range(2)]

                    def src_x1(b):
                        xt, b_xt = xb[b % 2]
                        P.dma("sp", lambda e: e.dma_start(out=xt[:], in_=x1_d[b * 128:(b + 1) * 128, :]),
                              reads=[b_x1d[b]], writes=[b_xt])
                        return xt[:], b_xt
                    norm_to_T(Ca, src_x1, scf, b_scf, lambda k: mod[:, 48 + k:48 + k + 1], b_mod, hf2, b_hf2, pT, b_pT)
                    P.emit()
                with ExitStack() as es4b:
                    Cb_ = Ctx(nc, es4b)
                    wg = [Cb_.sb([128, KC, 128], BF16) for _ in range(3)]
                    wu = [Cb_.sb([128, KC, 128], BF16) for _ in range(3)]
                    sl_ = [Cb_.sb([128, 512], F32) for _ in range(2)]
                    for f in range(NFF):
                        wgt, b_wg = wg[f % 3]
                        wut, b_wu = wu[f % 3]
                        P.dma("pool", lambda e, f=f, wgt=wgt: e.dma_start(
                            out=wgt[:], in_=wgu_d[:, f * 128:(f + 1) * 128].rearrange("(k p) n -> p k n", p=128)), writes=[b_wg])
                        P.dma("pool", lambda e, f=f, wut=wut: e.dma_start(
                            out=wut[:], in_=wgu_d[:, D_FF + f * 128:D_FF + (f + 1) * 128].rearrange("(k p) n -> p k n", p=128)),
                            writes=[b_wu])
                        for tg in range(TOK // 512):
                            pgt, b_pgt = pg_[(f * 2 + tg) % 2]
                            put, b_put = pu_[(f * 2 + tg) % 2]
                            s_, b_s = sl_[(f * 2 + tg) % 2]
                            P.op("pe", [(lambda e, k=k, wgt=wgt, pgt=pgt, tg=tg: e.matmul(
                                pgt[:], lhsT=wgt[:, k, :], rhs=hf2[:, k, tg * 512:(tg + 1) * 512],
                                start=(k == 0), stop=(k == KC - 1))) for k in range(KC)],
                                reads=[b_wg, b_hf2], writes=[b_pgt])
                            P.op("pe", [(lambda e, k=k, wut=wut, put=put, tg=tg: e.matmul(
                                put[:], lhsT=wut[:, k, :], rhs=hf2[:, k, tg * 512:(tg + 1) * 512],
                                start=(k == 0), stop=(k == KC - 1))) for k in range(KC)],
                                reads=[b_wu, b_hf2], writes=[b_put])
                            P.op("act", lambda e, s_=s_, pgt=pgt: e.activation(out=s_[:], in_=pgt[:], func=AF.Silu),
                                 reads=[b_pgt], writes=[b_s])
                            P.op("dve", lambda e, s_=s_, put=put, f=f, tg=tg: e.tensor_tensor(
                                out=actT[:, f, tg * 512:(tg + 1) * 512], in0=put[:], in1=s_[:], op=ALU.mult),
                                reads=[b_put, b_s], writes=[b_actT])
                    P.emit()
            with ExitStack() as es4c:
                Cc = Ctx(nc, es4c)
                DW = 512
                x1s = [Cc.sb([128, DW], F32) for _ in range(3)]
                t2 = [Cc.sb([128, DW], F32) for _ in range(2)]
                psD = [Cc.ps([128, 512], F32) for _ in range(2)]

                def ep_down(b, cb, outs):
                    pp, b_pp = outs[0]
                    xs_, b_xs = x1s[(cb * NB + b) % 3]
                    t_, b_t = t2[(cb * NB + b) % 2]
                    P.dma("sp", lambda e: e.dma_start(out=xs_[:], in_=x1_d[b * 128:(b + 1) * 128, cb * DW:(cb + 1) * DW]),
                          reads=[b_x1d[b]], writes=[b_xs])
                    P.op("dve", lambda e: e.tensor_tensor(out=t_[:], in0=pp[:, 0:DW], in1=gf_row[:, cb * DW:(cb + 1) * DW],
                                                          op=ALU.mult), reads=[b_pp, b_gf], writes=[b_t])
                    P.op("dve", lambda e: e.tensor_tensor(out=xs_[:], in0=xs_[:], in1=t_[:], op=ALU.add),
                         reads=[b_t, b_xs], writes=[b_xs])
                    P.dma("sp", lambda e: e.dma_start(out=x2_d[b * 128:(b + 1) * 128, cb * DW:(cb + 1) * DW], in_=xs_[:]),
                          reads=[b_xs], writes=[b_x2d[b][cb]])
                gemm_tok(Cc, [(actT, b_actT, (lambda cb: wd_d[:, cb * DW:(cb + 1) * DW]), NFF)], D // DW, DW, ep_down, [psD])
                P.wait_all("sp", [bb for row in b_x2d for bb in row[:D // DW]])
                P.emit()
        with ExitStack() as es5:
            C = Ctx(nc, es5)
            nfr, b_nfr = C.sb([128, D], F32)
            P.dma("sp", lambda e: e.dma_start(out=nfr[:], in_=nfin_d.partition_broadcast(128)), writes=[b_nfr])
            xb = [C.sb([128, D], F32) for _ in range(3)]
            junk, b_junk = C.sb([128, D], BF16)
            st, b_st = C.sb([128, NB, 3], F32)
            def load_x2(b):
                xt, b_xt = xb[b % 3]
                P.dma("sp", lambda e: e.dma_start(out=xt[:], in_=x2_d[b * 128:(b + 1) * 128, :]),
                      reads=b_x2d[b], writes=[b_xt])
            load_x2(0)
            for b in range(NB):
                xt, b_xt = xb[b % 3]
                if b + 1 < NB:
                    load_x2(b + 1)
                P.op("act", lambda e, b=b, xt=xt: e.activation(out=junk[:], in_=xt[:], func=AF.Square, accum_out=st[:, b, 0:1]),
                     reads=[b_xt], writes=[b_junk, b_st])
                P.op("act", lambda e, b=b: e.activation(out=st[:, b, 1:2], in_=st[:, b, 0:1], func=AF.Sqrt, scale=1.0 / D, bias=EPS),
                     reads=[b_st], writes=[b_st])
                P.op("dve", lambda e, b=b: e.reciprocal(out=st[:, b, 2:3], in_=st[:, b, 1:2]), reads=[b_st], writes=[b_st])
                P.op("dve", lambda e, b=b, xt=xt: e.scalar_tensor_tensor(out=xt[:], in0=xt[:], scalar=st[:, b, 2:3], in1=nfr[:],
                                                                         op0=ALU.mult, op1=ALU.mult),
                     reads=[b_st, b_xt, b_nfr], writes=[b_xt])
                ob = Buf()
                outbufs.append(ob)
                P.dma("sp", lambda e, xt=xt, b=b: e.dma_start(out=out_d[b * 128:(b + 1) * 128, :], in_=xt[:]),
                      reads=[b_xt], writes=[ob])
            P.wait_all("sp", outbufs)
            P.emit()
    return nc


def prep_C_inputs(inp, r, HAF, HAB, OBF, OBB, modfm):
    w_in = inp["w_in"][0]
    sl = slice(r * TOK, (r + 1) * TOK)
    w3 = np.concatenate([w_in[:, 4096:6144], w_in[:, 12304:14352], w_in[:, 14416:18512]], axis=1)
    return {
        "x": np.ascontiguousarray(inp["x"][0][sl]),
        "modfm": modfm,
        "norm_mix_w": np.ascontiguousarray(inp["norm_mix_w"][0]),
        "norm_ffn_w": np.ascontiguousarray(inp["norm_ffn_w"][0]),
        "norm_final_w": np.ascontiguousarray(inp["norm_final_w"]),
        "mlstm_norm_w": np.ascontiguousarray(inp["mlstm_norm_w"][0]),
        "gdn_norm_w": np.ascontiguousarray(inp["gdn_norm_w"][0]),
        "w3": np.ascontiguousarray(w3),
        "w_branch_a": np.ascontiguousarray(inp["w_branch_a"][0]),
        "w_branch_b": np.ascontiguousarray(inp["w_branch_b"][0]),
        "w_out": np.ascontiguousarray(inp["w_out"][0]),
        "w_gate_up": np.ascontiguousarray(inp["w_gate_up"][0]),
        "w_down": np.ascontiguousarray(inp["w_down"][0]),
        "haf": np.ascontiguousarray(HAF[sl]),
        "hab": np.ascontiguousarray(HAB[sl]),
        "obf": np.ascontiguousarray(OBF[sl]),
        "obb": np.ascontiguousarray(OBB[sl]),
    }


def build_AB():
    nc = bass.Bass("TRN2", target_bir_lowering=False)
    with ExitStack() as es:
        sems = [es.enter_context(nc.semaphore(f"s{i}")) for i in range(100)]
        sh = {"nc": nc, "P": Prog(nc, sems), "scr": {}}
        build_A(sh=sh)
        build_B(sh=sh)
    return nc


def kernel(**inputs):
    inp = {k: np.asarray(v) for k, v in inputs.items()}
    cores = list(range(NCORE))
    ncAB = build_AB()
    resB = run_bass_kernel_spmd(ncAB, [prep_A_inputs(inp, r) for r in cores], core_ids=cores)
    HAF = np.concatenate([np.asarray(resB.results[r]["hab"])[:, 0:256] for r in cores], axis=1)
    HAB = np.concatenate([np.asarray(resB.results[r]["hab"])[:, 256:512] for r in cores], axis=1)
    OBF = np.concatenate([np.asarray(resB.results[r]["hab"])[:, 512:768] for r in cores], axis=1)
    OBB = np.concatenate([np.asarray(resB.results[r]["hab"])[:, 768:1024] for r in cores], axis=1)
    mo = [np.asarray(resB.results[r]["modout"]) for r in cores]
    modfm = np.ascontiguousarray(np.concatenate([mo[0][:, 0:32]] + [m[:, 32:40] for m in mo], axis=1))
    ncC = build_C()
    resC = run_bass_kernel_spmd(ncC, [prep_C_inputs(inp, r, HAF, HAB, OBF, OBB, modfm) for r in cores], core_ids=cores)
    out = np.concatenate([np.asarray(resC.results[r]["out"]) for r in cores], axis=0)
    return out.reshape(1, T, D).astype(np.float32)
```

```python
from contextlib import ExitStack
import numpy as np
import concourse.bass as bass
import concourse.mybir as mybir
from concourse.bass_utils import run_bass_kernel_spmd

F32 = mybir.dt.float32
BF16 = mybir.dt.bfloat16
ALU = mybir.AluOpType
AF = mybir.ActivationFunctionType
AX = mybir.AxisListType

D = 2048
T = 8192
NCORE = 8
KC = D // 128
EPS = 1e-6
A_DK = 256
B_DK = 128
NBLK = T // 128
NGRP = T // 512
import os
DBG = int(os.environ.get('DBG', '99'))
SKIP_SELF = int(os.environ.get('SKIP_SELF', '0'))
USE_F32R = int(os.environ.get('F32R', '1'))
F32R = mybir.dt.float32r if USE_F32R else F32


class Buf:
    __slots__ = ("name", "w", "r", "excl")

    def __init__(self, name="", excl=False):
        self.name = name
        self.w = None
        self.r = []
        self.excl = excl


ENGS = ("pe", "act", "dve", "pool", "sp")
EPOCH = 24000
NDMA = 24


class Prog:
    def __init__(self, nc, sems):
        self.nc = nc
        self.free = list(sems)
        self.ins = {e: [] for e in ENGS}
        self.cnt = {e: 0 for e in ENGS}
        self.sem = {e: self.free.pop() for e in ENGS}
        self.waited = {e: {} for e in ENGS}
        self.dsem = {"hw": [self.free.pop() for _ in range(NDMA)], "sw": [self.free.pop() for _ in range(NDMA)]}
        self.dval = {"hw": [0] * NDMA, "sw": [0] * NDMA}
        self.dnext = {"hw": 0, "sw": 0}
        self.semobj = {}
        self.own = {}
        for e in ENGS:
            self.semobj[id(self.sem[e])] = self.sem[e]
            self.own[id(self.sem[e])] = e
        for pool_ in self.dsem.values():
            for s in pool_:
                self.semobj[id(s)] = s

    def _need(self, eng, toks):
        best = {}
        for t in toks:
            if t is None:
                continue
            s, v = t
            k = id(s)
            if SKIP_SELF and eng in ("pe", "act", "dve", "pool") and self.own.get(k) == eng:
                continue
            if self.waited[eng].get(k, 0) >= v:
                continue
            if best.get(k, 0) < v:
                best[k] = v
        out = []
        for k, v in best.items():
            self.waited[eng][k] = v
            out.append((self.semobj[k], v))
        return out

    @staticmethod
    def _deps(reads, writes):
        toks = []
        for b in reads:
            toks.append(b.w)
        for b in writes:
            toks.append(b.w)
            toks.extend(b.r)
        return toks

    @staticmethod
    def _mark(tok, reads, writes):
        for b in reads:
            b.r.append(tok)
        for b in writes:
            b.w = tok
            b.r = []

    def op(self, eng, fns, reads=(), writes=()):
        if callable(fns):
            fns = [fns]
        if any(b.excl for b in reads):
            writes = list(writes) + [b for b in reads if b.excl]
            reads = [b for b in reads if not b.excl]
        waits = self._need(eng, self._deps(reads, writes))
        if self.cnt[eng] >= EPOCH:
            s = self.free.pop()
            self.semobj[id(s)] = s
            self.own[id(s)] = eng
            self.sem[eng] = s
            self.cnt[eng] = 0
        self.cnt[eng] += 1
        tok = (self.sem[eng], self.cnt[eng])
        self.ins[eng].append((waits, fns, (self.sem[eng], 1)))
        self._mark(tok, reads, writes)
        return tok

    def dma(self, q, fn, reads=(), writes=()):
        kind = "sw" if q == "pool" else "hw"
        j = self.dnext[kind]
        self.dnext[kind] = (j + 1) % NDMA
        s = self.dsem[kind][j]
        toks = self._deps(reads, writes)
        if self.dval[kind][j] > 0:
            toks.append((s, self.dval[kind][j]))
        waits = self._need(q, toks)
        self.dval[kind][j] += 16
        tok = (s, self.dval[kind][j])
        self.ins[q].append((waits, [fn], (s, 16)))
        self._mark(tok, reads, writes)
        return tok

    def wait_all(self, eng, bufs):
        waits = self._need(eng, [b.w for b in bufs])
        self.ins[eng].append((waits, [], None))

    def emit(self):
        nc = self.nc
        ins = self.ins
        if not any(ins[e] for e in ENGS):
            return
        with nc.Block() as block:
            def body(ename):
                def f(e):
                    for waits, fns, inc in ins[ename]:
                        for s, v in waits:
                            e.wait_ge(s, v)
                        last = None
                        for fn in fns:
                            last = fn(e)
                        if inc is not None and last is not None:
                            last.then_inc(inc[0], inc[1])
                return f
            block.tensor(body("pe"))
            block.scalar(body("act"))
            block.vector(body("dve"))
            block.gpsimd(body("pool"))
            block.sync(body("sp"))
        self.ins = {e: [] for e in ENGS}


class Ctx:
    n = 0

    def __init__(self, nc, es):
        self.nc = nc
        self.es = es

    def sb(self, shape, dt, name=None):
        Ctx.n += 1
        nm = f"{name or 'sb'}_{Ctx.n}"
        t = self.es.enter_context(self.nc.sbuf_tensor(nm, list(shape), dt))
        return t, Buf(nm)

    def ps(self, shape, dt, name=None):
        Ctx.n += 1
        nm = f"{name or 'ps'}_{Ctx.n}"
        t = self.es.enter_context(self.nc.psum_tensor(nm, list(shape), dt))
        return t, Buf(nm, excl=True)


def make_identity(P, C, dt=BF16):
    idf, b_idf = C.sb([128, 128], F32)
    ident, b_id = C.sb([128, 128], dt)
    P.op("pool", lambda e: e.memset(idf[:], 0.0), writes=[b_idf])
    P.op("pool", lambda e: e.affine_select(out=idf[:], in_=idf[:], pattern=[[-1, 128]],
                                           compare_op=ALU.not_equal, fill=1.0, base=0, channel_multiplier=1),
         reads=[b_idf], writes=[b_idf])
    P.op("dve", lambda e: e.tensor_copy(out=ident[:], in_=idf[:]), reads=[b_idf], writes=[b_id])
    return ident, b_id, idf, b_idf


def compute_mod_cols(P, C, nc, c_d, adaw_d, adab_d, ncols, mod_sb, b_mod):
    nj = ncols // 128
    c_sb, b_c = C.sb([128, KC], F32)
    cact, b_cact = C.sb([128, KC], BF16)
    ab, b_ab = C.sb([128, nj], F32)
    pm, b_pm = C.ps([128, nj], F32)
    P.dma("sp", lambda e: e.dma_start(out=c_sb[:], in_=c_d.rearrange("(k p) -> p k", p=128), allow_slow_non_contiguous=True), writes=[b_c])
    P.dma("sp", lambda e: e.dma_start(out=ab[:], in_=adab_d.rearrange("(j p) -> p j", p=128), allow_slow_non_contiguous=True), writes=[b_ab])
    P.op("act", lambda e: e.activation(out=cact[:], in_=c_sb[:], func=AF.Silu), reads=[b_c], writes=[b_cact])
    CH = 1024
    wbuf = [C.sb([128, KC, CH], BF16) for _ in range(2)]
    for ci in range(ncols // CH):
        wt, b_wt = wbuf[ci % 2]
        P.dma("pool", lambda e, wt=wt, ci=ci: e.dma_start(
            out=wt[:], in_=adaw_d[:, ci * CH:(ci + 1) * CH].rearrange("(k p) n -> p k n", p=128)), writes=[b_wt])
        for jj in range(CH // 128):
            j = ci * (CH // 128) + jj
            P.op("pe", [(lambda e, wt=wt, jj=jj, j=j, k=k: e.matmul(
                pm[:, j:j + 1], lhsT=wt[:, k, jj * 128:(jj + 1) * 128], rhs=cact[:, k:k + 1],
                start=(k == 0), stop=(k == KC - 1))) for k in range(KC)],
                reads=[b_wt, b_cact], writes=[b_pm])
    P.op("dve", lambda e: e.tensor_tensor(out=mod_sb[:, 0:nj], in0=pm[:], in1=ab[:], op=ALU.add),
         reads=[b_pm, b_ab], writes=[b_mod])


NF = 1280
NT = 268


def build_A(ngrp=NGRP, sh=None):
    nc = sh["nc"] if sh else bass.Bass("TRN2", target_bir_lowering=False)
    dt_in = lambda name, shape: nc.dram_tensor(name, shape, F32, kind="ExternalInput").ap()
    x_d = dt_in("x", [T, D])
    c_d = dt_in("c", [D])
    adaw_d = dt_in("ada_w2", [D, 2 * D + 1024])
    adab_d = dt_in("ada_b2", [2 * D + 1024])
    modout_d = nc.dram_tensor("modout", [128, 40], F32, kind="ExternalOutput").ap()
    nw_d = dt_in("norm_mix_w", [D])
    w1f_d = dt_in("w1f", [D, NF])
    w1t_d = dt_in("w1t", [D, NT])
    gbias_d = dt_in("gate_bias", [12])
    gA_d = dt_in("gate_A", [4])
    cw_d = dt_in("conv_w", [768, 5])
    def out(name, shape, dt):
        ap = nc.dram_tensor(name, shape, dt, kind=("Internal" if sh else "ExternalOutput")).ap()
        if sh:
            sh["scr"][name] = ap
        return ap
    qta_d = out("qta", [256, T], BF16)
    kta_d = out("kta", [256, T], BF16)
    ka_d = out("ka", [T, 256], BF16)
    va_d = out("va", [T, 256], BF16)
    qtb_d = out("qtb", [256, T], BF16)
    ktb_d = out("ktb", [256, T], BF16)
    kb_d = out("kb", [T, 256], BF16)
    vb_d = out("vb", [T, 256], BF16)
    gts_d = out("gts", [T, 12], F32)
    outbufs = []

    with ExitStack() as es0:
        if sh:
            P = sh["P"]
        else:
            sems = [es0.enter_context(nc.semaphore(f"s{i}")) for i in range(100)]
            P = Prog(nc, sems)
        C0 = Ctx(nc, es0)
        sc, b_sc = C0.sb([128, KC], F32, "sc")
        sh, b_sh = C0.sb([128, KC], F32, "sh")
        with ExitStack() as es1:
            C = Ctx(nc, es1)
            mod, b_mod = C.sb([128, 40], F32)
            nw, b_nw = C.sb([128, KC], F32)
            P.dma("sp", lambda e: e.dma_start(out=nw[:], in_=nw_d.rearrange("(k p) -> p k", p=128), allow_slow_non_contiguous=True), writes=[b_nw])
            compute_mod_cols(P, C, nc, c_d, adaw_d, adab_d, 2 * D + 1024, mod, b_mod)
            ob = Buf()
            outbufs.append(ob)
            P.dma("sp", lambda e: e.dma_start(out=modout_d, in_=mod[:]), reads=[b_mod], writes=[ob])
            P.op("dve", lambda e: e.scalar_tensor_tensor(out=sc[:], in0=mod[:, 16:32], scalar=1.0, in1=nw[:],
                                                         op0=ALU.add, op1=ALU.mult),
                 reads=[b_mod, b_nw], writes=[b_sc])
            P.op("dve", lambda e: e.tensor_copy(out=sh[:], in_=mod[:, 0:16]), reads=[b_mod], writes=[b_sh])
            P.wait_all("sp", [ob])
            P.emit()
        with ExitStack() as es2:
            C = Ctx(nc, es2)
            ident, b_id, idf, b_idf = make_identity(P, C)
            ones32, b_ones32 = C.sb([128, 128], F32)
            ones_f, b_ones = C.sb([128, 128], F32R)
            P.op("pool", lambda e: e.memset(ones32[:], 1.0), writes=[b_ones32])
            P.op("dve", lambda e: e.tensor_copy(out=ones_f[:], in_=ones32[:]), reads=[b_ones32], writes=[b_ones])
            w1f, b_w1f = C.sb([128, KC, NF], BF16)
            w1t, b_w1t = C.sb([128, KC, NT], BF16)
            for q in range(4):
                P.dma("pool", lambda e, q=q: e.dma_start(
                    out=w1f[:, :, q * 320:(q + 1) * 320],
                    in_=w1f_d[:, q * 320:(q + 1) * 320].rearrange("(k p) n -> p k n", p=128)), writes=[b_w1f])
            P.dma("pool", lambda e: e.dma_start(out=w1t[:], in_=w1t_d.rearrange("(k p) n -> p k n", p=128)),
                  writes=[b_w1t])
            cw, b_cw = C.sb([128, 6, 5], F32)
            P.dma("sp", lambda e: e.dma_start(out=cw[:], in_=cw_d.rearrange("(m p) k -> p m k", p=128)), writes=[b_cw])
            gb, b_gb = C.sb([128, 12], F32)
            P.dma("sp", lambda e: e.dma_start(out=gb[:], in_=gbias_d.partition_broadcast(128)), writes=[b_gb])
            negA, b_negA = C.sb([128, 4], F32)
            P.dma("sp", lambda e: e.dma_start(out=negA[:], in_=gA_d.partition_broadcast(128)), writes=[b_negA])
            P.op("act", lambda e: e.activation(out=negA[:], in_=negA[:], func=AF.Exp), reads=[b_negA], writes=[b_negA])
            P.op("dve", lambda e: e.tensor_scalar(out=negA[:], in0=negA[:], scalar1=-1.0, scalar2=None, op0=ALU.mult),
                 reads=[b_negA], writes=[b_negA])

            NXB = 3
            xbuf = [C.sb([128, D], F32) for _ in range(NXB)]
            junk, b_junk = C.sb([128, D], BF16)
            stat, b_stat = C.sb([128, NBLK, 3], F32)
            xsb = [C.sb([128, D], BF16) for _ in range(2)]
            hT = [C.sb([128, KC, 512], BF16) for _ in range(2)]
            pT, b_pT = C.ps([128, KC, 128], BF16)
            pF = [C.ps([128, 512], F32) for _ in range(2)]
            pTm = [C.ps([128, 512], F32) for _ in range(2)]
            pX = [C.ps([128, 1024], BF16) for _ in range(2)]
            raw2 = [[C.sb([128, 516], F32) for _ in range(6)] for _ in range(2)]
            for pp_ in range(2):
                for m in range(6):
                    P.op("pool", lambda e, t=raw2[pp_][m][0]: e.memset(t[:], 0.0), writes=[raw2[pp_][m][1]])
            acc = [C.sb([128, 512], F32) for _ in range(6)]
            sil = [C.sb([128, 512], F32) for _ in range(6)]
            rn = [C.sb([128, 512], F32) for _ in range(4)]
            sqr = [C.sb([128, 512], F32R) for _ in range(4)]
            fo = [C.sb([128, 512], BF16) for _ in range(12)]
            fk = [[C.sb([128, 512], BF16) for _ in range(2)] for _ in range(2)]
            to = [C.sb([128, 4, 128], BF16) for _ in range(4)]
            vo = [C.sb([128, 256], BF16) for _ in range(4)]
            z4, b_z4 = C.sb([128, 4, 12], F32)
            e4, b_e4 = C.sb([128, 4, 10], F32)
            g4 = [C.sb([128, 4, 12], F32) for _ in range(2)]
            cnt = {"fo": 0, "to": 0, "pF": 0, "pX": 0, "acc": 0}

            def rr(key, lst):
                i = cnt[key]
                cnt[key] += 1
                return lst[i % len(lst)]

            def load_x(n):
                xt, b_xt = xbuf[n % NXB]
                P.dma("sp", lambda e: e.dma_start(out=xt[:], in_=x_d[n * 128:(n + 1) * 128, :]), writes=[b_xt])

            def norm_block(n, hTg, b_hTg):
                xt, b_xt = xbuf[n % NXB]
                xs, b_xs = xsb[n % 2]
                P.op("act", lambda e: e.activation(out=junk[:], in_=xt[:], func=AF.Square,
                                                   accum_out=stat[:, n, 0:1]),
                     reads=[b_xt], writes=[b_junk, b_stat])
                P.op("act", lambda e: e.activation(out=stat[:, n, 1:2], in_=stat[:, n, 0:1], func=AF.Ln,
                                                   scale=1.0 / D, bias=EPS), reads=[b_stat], writes=[b_stat])
                P.op("act", lambda e: e.activation(out=stat[:, n, 2:3], in_=stat[:, n, 1:2], func=AF.Exp, scale=-0.5),
                     reads=[b_stat], writes=[b_stat])
                P.op("dve", lambda e: e.tensor_scalar(out=xs[:], in0=xt[:], scalar1=stat[:, n, 2:3], scalar2=None,
                                                      op0=ALU.mult), reads=[b_stat, b_xt], writes=[b_xs])
                P.op("pe", [(lambda e, k=k: e.transpose(out=pT[:, k, :], in_=xs[:, k * 128:(k + 1) * 128],
                                                        identity=ident[:])) for k in range(KC)],
                     reads=[b_xs, b_id], writes=[b_pT])
                o = (n % 4) * 128
                P.op("act", [(lambda e, k=k: e.activation(out=hTg[:, k, o:o + 128], in_=pT[:, k, :], func=AF.Identity,
                                                          scale=sc[:, k:k + 1], bias=sh[:, k:k + 1]))
                             for k in range(KC)],
                     reads=[b_pT, b_sc, b_sh], writes=[b_hTg])

            def store_fm(dst, rows, tlo, src_t, jlo, jhi, b_src):
                ob = Buf()
                outbufs.append(ob)
                P.dma("sp", lambda e: e.dma_start(out=dst[rows[0]:rows[1], tlo + jlo:tlo + jhi],
                                                    in_=src_t[:, jlo:jhi]), reads=[b_src], writes=[ob])

            def tm_from_fm(src_t, b_src, dst, col0, tlo, jlo, jhi):
                px, b_px = rr("pX", pX)
                tt, b_tt = rr("to", to)
                P.op("pe", [(lambda e, i=i: e.transpose(out=px[:, i * 128:(i + 1) * 128],
                                                        in_=src_t[:, i * 128:(i + 1) * 128], identity=ident[:]))
                            for i in range(4)], reads=[b_src, b_id], writes=[b_px])
                P.op("dve", lambda e: e.tensor_copy(out=tt[:].rearrange("p a b -> p (a b)"), in_=px[:, 0:512]),
                     reads=[b_px], writes=[b_tt])
                if jlo == 0 and jhi == 512:
                    ob = Buf()
                    outbufs.append(ob)
                    P.dma("sp", lambda e: e.dma_start(
                        out=dst[tlo:tlo + 512, col0:col0 + 128].rearrange("(i p) c -> p i c", p=128), in_=tt[:]),
                        reads=[b_tt], writes=[ob])
                    return
                for i in range(4):
                    lo = max(jlo, i * 128)
                    hi = min(jhi, (i + 1) * 128)
                    if lo >= hi:
                        continue
                    ob = Buf()
                    outbufs.append(ob)
                    P.dma("sp", lambda e, i=i, lo=lo, hi=hi: e.dma_start(
                        out=dst[tlo + lo:tlo + hi, col0:col0 + 128],
                        in_=tt[lo - i * 128:hi - i * 128, i, :]), reads=[b_tt], writes=[ob])

            def sigmoid_of(src, b_src, dst, b_dst):
                P.op("act", lambda e: e.activation(out=dst[:], in_=src[:], func=AF.Exp, scale=-1.0), reads=[b_src], writes=[b_dst])
                P.op("act", lambda e: e.activation(out=dst[:], in_=dst[:], func=AF.Ln, bias=1.0), reads=[b_dst], writes=[b_dst])
                P.op("act", lambda e: e.activation(out=dst[:], in_=dst[:], func=AF.Exp, scale=-1.0), reads=[b_dst], writes=[b_dst])

            def tm_gen(src_t, b_src, dst, col0, tlo, jlo, jhi):
                yield
                tm_from_fm(src_t, b_src, dst, col0, tlo, jlo, jhi)
                yield

            def gdn_post(g, m):
                rw, b_rw = raw2[g % 2][m]
                tlo = g * 512 - 2
                jlo = 2 if g == 0 else 0
                jhi = 2 if g == ngrp else 512
                ac, b_ac = acc[m]
                sl, b_sl = sil[m]
                P.op("dve", lambda e: e.tensor_scalar(out=ac[:], in0=rw[:, 0:512], scalar1=cw[:, m, 0:1], scalar2=None,
                                                      op0=ALU.mult), reads=[b_rw, b_cw], writes=[b_ac])
                for k in range(1, 5):
                    P.op("dve", lambda e, k=k: e.scalar_tensor_tensor(out=ac[:], in0=rw[:, k:k + 512],
                                                                      scalar=cw[:, m, k:k + 1], in1=ac[:],
                                                                      op0=ALU.mult, op1=ALU.add),
                         reads=[b_rw, b_cw, b_ac], writes=[b_ac])
                yield
                sigmoid_of(ac, b_ac, sl, b_sl)
                yield
                head = m % 2
                if m >= 4:
                    f, b_f = rr("fo", fo)
                    P.op("dve", lambda e: e.tensor_tensor(out=f[:], in0=ac[:], in1=sl[:], op=ALU.mult),
                         reads=[b_ac, b_sl], writes=[b_f])
                    yield
                    tm_from_fm(f, b_f, vb_d, head * 128, tlo, jlo, jhi)
                    yield
                    return
                P.op("dve", lambda e: e.tensor_tensor(out=sl[:], in0=ac[:], in1=sl[:], op=ALU.mult),
                     reads=[b_ac, b_sl], writes=[b_sl])
                s2t, b_s2 = sqr[m]
                s2 = s2t[:]
                r2, b_r2 = rn[m]
                P.op("dve", lambda e: e.tensor_tensor(out=s2, in0=sl[:], in1=sl[:], op=ALU.mult),
                     reads=[b_sl], writes=[b_s2])
                yield
                pf, b_pf = rr("pF", pF)
                P.op("pe", lambda e: e.matmul(pf[:], lhsT=ones_f[:], rhs=s2, start=True, stop=True),
                     reads=[b_s2, b_ones], writes=[b_pf])
                P.op("act", lambda e: e.activation(out=r2[:], in_=pf[:], func=AF.Ln, bias=EPS), reads=[b_pf], writes=[b_r2])
                P.op("act", lambda e: e.activation(out=r2[:], in_=r2[:], func=AF.Exp, scale=-0.5), reads=[b_r2], writes=[b_r2])
                yield
                f, b_f = rr("fo", fo)
                qs = (B_DK ** -0.5) if m < 2 else 1.0
                P.op("dve", lambda e: e.scalar_tensor_tensor(out=f[:], in0=sl[:], scalar=qs, in1=r2[:],
                                                             op0=ALU.mult, op1=ALU.mult),
                     reads=[b_sl, b_r2], writes=[b_f])
                dst = qtb_d if m < 2 else ktb_d
                store_fm(dst, (head * 128, head * 128 + 128), tlo, f, jlo, jhi, b_f)
                yield
                if m >= 2:
                    tm_from_fm(f, b_f, kb_d, head * 128, tlo, jlo, jhi)
                yield

            def unit_norm(n):
                g_ = n // 4
                hTg, b_hTg = hT[g_ % 2]
                norm_block(n, hTg, b_hTg)
                if n + NXB < 4 * ngrp:
                    load_x(n + NXB)

            pending = []

            def unit_feat(g, mb):
                hTg, b_hTg = hT[g % 2]
                pf, b_pf = rr("pF", pF)
                P.op("pe", [(lambda e, k=k: e.matmul(
                    pf[:], lhsT=w1f[:, k, mb * 128:(mb + 1) * 128], rhs=hTg[:, k, :],
                    start=(k == 0), stop=(k == KC - 1))) for k in range(KC)],
                    reads=[b_w1f, b_hTg], writes=[b_pf])
                if mb < 4:
                    f, b_f = rr("fo", fo) if mb < 2 else fk[g % 2][mb - 2]
                    if mb < 2:
                        P.op("act", lambda e: e.activation(out=f[:], in_=pf[:], func=AF.Copy, scale=A_DK ** -0.5),
                             reads=[b_pf], writes=[b_f])
                        store_fm(qta_d, (mb * 128, mb * 128 + 128), g * 512, f, 0, 512, b_f)
                    else:
                        P.op("act", lambda e: e.activation(out=f[:], in_=pf[:], func=AF.Copy),
                             reads=[b_pf], writes=[b_f])
                        store_fm(kta_d, ((mb - 2) * 128, (mb - 2) * 128 + 128), g * 512, f, 0, 512, b_f)
                        newgens.append(tm_gen(f, b_f, ka_d, (mb - 2) * 128, g * 512, 0, 512))
                else:
                    m = mb - 4
                    rw, b_rw = raw2[g % 2][m]
                    rwp, b_rwp = raw2[(g - 1) % 2][m]
                    P.op("act", lambda e: e.activation(out=rw[:, 4:516], in_=pf[:], func=AF.Copy),
                         reads=[b_pf], writes=[b_rw])
                    P.op("act", lambda e: e.activation(out=rw[:, 0:4], in_=rwp[:, 512:516], func=AF.Copy),
                         reads=[b_rwp], writes=[b_rw])
                    newgens.append(gdn_post(g, m))

            def step_pending(nsteps=1):
                for _ in range(nsteps):
                    alive = []
                    for g_ in pending:
                        try:
                            next(g_)
                            alive.append(g_)
                        except StopIteration:
                            pass
                    pending[:] = alive

            def unit_tok(g, bl):
                hTg, b_hTg = hT[g % 2]
                pt, b_pt = pTm[bl % 2]
                P.op("pe", [(lambda e, k=k: e.matmul(
                    pt[:, 0:NT], lhsT=hTg[:, k, bl * 128:(bl + 1) * 128], rhs=w1t[:, k, :],
                    start=(k == 0), stop=(k == KC - 1))) for k in range(KC)],
                    reads=[b_w1t, b_hTg], writes=[b_pt])
                v, b_v = vo[bl % 4]
                P.op("act", lambda e: e.activation(out=v[:], in_=pt[:, 0:256], func=AF.Copy),
                     reads=[b_pt], writes=[b_v])
                ob = Buf()
                outbufs.append(ob)
                n = g * 4 + bl
                P.dma("sp", lambda e: e.dma_start(out=va_d[n * 128:(n + 1) * 128, :], in_=v[:]),
                      reads=[b_v], writes=[ob])
                P.op("dve", lambda e: e.tensor_tensor(out=z4[:, bl, :], in0=pt[:, 256:268], in1=gb[:], op=ALU.add),
                     reads=[b_gb, b_pt], writes=[b_z4])

            def unit_gates(g):
                gg, b_gg = g4[g % 2]
                P.op("dve", lambda e: e.tensor_copy(out=gg[:, :, 0:2], in_=z4[:, :, 0:2]), reads=[b_z4], writes=[b_gg])
                P.op("act", lambda e: e.activation(out=e4[:, :, 0:6], in_=z4[:, :, 2:8], func=AF.Exp, scale=-1.0),
                     reads=[b_z4], writes=[b_e4])
                P.op("act", lambda e: e.activation(out=e4[:, :, 6:10], in_=z4[:, :, 8:12], func=AF.Exp),
                     reads=[b_z4], writes=[b_e4])
                P.op("act", lambda e: e.activation(out=gg[:, :, 2:4], in_=e4[:, :, 0:2], func=AF.Ln, bias=1.0),
                     reads=[b_e4], writes=[b_gg])
                P.op("dve", lambda e: e.tensor_scalar(out=gg[:, :, 2:4], in0=gg[:, :, 2:4], scalar1=-1.0,
                                                      scalar2=None, op0=ALU.mult), reads=[b_gg], writes=[b_gg])
                P.op("dve", lambda e: e.tensor_scalar(out=e4[:, :, 2:6], in0=e4[:, :, 2:6], scalar1=1.0,
                                                      scalar2=None, op0=ALU.add), reads=[b_e4], writes=[b_e4])
                P.op("dve", lambda e: e.reciprocal(out=gg[:, :, 4:8], in_=e4[:, :, 2:6]), reads=[b_e4], writes=[b_gg])
                P.op("act", lambda e: e.activation(out=gg[:, :, 8:12], in_=e4[:, :, 6:10], func=AF.Ln, bias=1.0),
                     reads=[b_e4], writes=[b_gg])
                for bl in range(4):
                    P.op("dve", lambda e, bl=bl: e.tensor_tensor(out=gg[:, bl, 8:12], in0=gg[:, bl, 8:12],
                                                                 in1=negA[:], op=ALU.mult),
                         reads=[b_gg, b_negA], writes=[b_gg])
                ob = Buf()
                outbufs.append(ob)
                P.dma("sp", lambda e: e.dma_start(
                    out=gts_d[g * 512:(g + 1) * 512, :].rearrange("(b p) c -> p b c", p=128), in_=gg[:]),
                    reads=[b_gg], writes=[ob])

            for n in range(min(NXB, 4 * ngrp)):
                load_x(n)
            for bl in range(4):
                unit_norm(bl)
            newgens = []
            for g in range(ngrp):
                units = [(lambda mb=mb: unit_feat(g, mb)) for mb in range(10)]
                units += [(lambda bl=bl: unit_tok(g, bl)) for bl in range(4)]
                units.append(lambda: unit_gates(g))
                norms = [(lambda n=n: unit_norm(n)) for n in range(4 * (g + 1), 4 * (g + 2))] if g + 1 < ngrp else []
                for i, u in enumerate(units):
                    u()
                    step_pending(1)
                    if i in (1, 4, 7, 10) and norms:
                        norms.pop(0)()
                for nn in norms:
                    nn()
                while pending:
                    step_pending(1)
                pending.extend(newgens)
                newgens = []
            for m in range(6):
                rw, b_rw = raw2[ngrp % 2][m]
                rwp, b_rwp = raw2[(ngrp - 1) % 2][m]
                P.op("pool", lambda e, rw=rw: e.memset(rw[:, 4:516], 0.0), writes=[b_rw])
                P.op("act", lambda e, rw=rw, rwp=rwp: e.activation(out=rw[:, 0:4], in_=rwp[:, 512:516], func=AF.Copy),
                     reads=[b_rwp], writes=[b_rw])
                newgens.append(gdn_post(ngrp, m))
            while pending:
                step_pending(1)
            pending.extend(newgens)
            while pending:
                step_pending(1)
            P.wait_all("sp", outbufs)
            P.emit()
    return nc


def prep_A_inputs(inp, r):
    w_in = inp["w_in"][0]
    ha, hv = r // 2, r % 2
    hb = (2 * r, 2 * r + 1)
    o_aq, o_ak, o_av = 0, 1024, 2048
    o_ai, o_af = 6144, 6152
    o_bq = 6160
    o_bk = o_bq + 2048
    o_bv = o_bq + 4096
    o_beta, o_ba = 14352, 14384
    cols_f = list(range(o_aq + ha * 256, o_aq + ha * 256 + 256)) + list(range(o_ak + ha * 256, o_ak + ha * 256 + 256))
    for base in (o_bq, o_bk, o_bv):
        for h in hb:
            cols_f += list(range(base + h * 128, base + h * 128 + 128))
    cols_t = list(range(o_av + ha * 512 + hv * 256, o_av + ha * 512 + hv * 256 + 256))
    cols_t += [o_ai + 0 * 4 + ha, o_ai + 1 * 4 + ha, o_af + 0 * 4 + ha, o_af + 1 * 4 + ha]
    cols_t += [o_beta + 0 * 16 + hb[0], o_beta + 0 * 16 + hb[1], o_beta + 1 * 16 + hb[0], o_beta + 1 * 16 + hb[1]]
    cols_t += [o_ba + 0 * 16 + hb[0], o_ba + 0 * 16 + hb[1], o_ba + 1 * 16 + hb[0], o_ba + 1 * 16 + hb[1]]
    bi, bf = inp["mlstm_b_i"][0], inp["mlstm_b_f"][0]
    dtb, Al = inp["gdn_dt_bias"][0], inp["gdn_A_log"][0]
    gbias = np.array([bi[0, ha], bi[1, ha], bf[0, ha], bf[1, ha], 0, 0, 0, 0,
                      dtb[0, hb[0]], dtb[0, hb[1]], dtb[1, hb[0]], dtb[1, hb[1]]], np.float32)
    gA = np.array([Al[0, hb[0]], Al[0, hb[1]], Al[1, hb[0]], Al[1, hb[1]]], np.float32)
    cwt = inp["gdn_conv_w"][0]
    ccols = []
    for base in (0, 2048, 4096):
        for h in hb:
            ccols += list(range(base + h * 128, base + h * 128 + 128))
    conv_w = np.ascontiguousarray(cwt[:, ccols].T)
    return {
        "x": np.ascontiguousarray(inp["x"][0]),
        "c": np.ascontiguousarray(inp["c"][0]),
        "ada_w2": np.ascontiguousarray(np.concatenate(
            [inp["ada_w"][0][:, 0:2 * D], inp["ada_w"][0][:, 2 * D + r * 1024: 2 * D + (r + 1) * 1024]], axis=1)),
        "ada_b2": np.ascontiguousarray(np.concatenate(
            [inp["ada_b"][0][0:2 * D], inp["ada_b"][0][2 * D + r * 1024: 2 * D + (r + 1) * 1024]])),
        "norm_mix_w": np.ascontiguousarray(inp["norm_mix_w"][0]),
        "w1f": np.ascontiguousarray(w_in[:, cols_f]),
        "w1t": np.ascontiguousarray(w_in[:, cols_t]),
        "gate_bias": gbias,
        "gate_A": gA,
        "conv_w": conv_w,
    }


NEGBIG = 30000.0


def FR(ap):
    return ap


def F(ap):
    return ap.bitcast(F32) if USE_F32R else ap


def build_B(nchunk=128, sh=None):
    nc = sh["nc"] if sh else bass.Bass("TRN2", target_bir_lowering=False)

    def din(name, shape, dt):
        if sh:
            return sh["scr"][name]
        return nc.dram_tensor(name, shape, dt, kind="ExternalInput").ap()
    qta_d = din("qta", [256, T], BF16)
    kta_d = din("kta", [256, T], BF16)
    ka_d = din("ka", [T, 256], BF16)
    va_d = din("va", [T, 256], BF16)
    qtb_d = din("qtb", [256, T], BF16)
    ktb_d = din("ktb", [256, T], BF16)
    kb_d = din("kb", [T, 256], BF16)
    vb_d = din("vb", [T, 256], BF16)
    gts_d = din("gts", [T, 12], F32)
    hab_d = nc.dram_tensor("hab", [T, 1024], F32, kind="ExternalOutput").ap()
    outbufs = []

    with ExitStack() as es:
        if sh:
            P = sh["P"]
        else:
            sems = [es.enter_context(nc.semaphore(f"s{i}")) for i in range(100)]
            P = Prog(nc, sems)
        C = Ctx(nc, es)
        ident_b, b_idb, idf, b_idf = make_identity(P, C)
        identf = idf[0:64, 0:64]
        idr_t, b_idr = C.sb([64, 64], F32R)
        P.op("dve", lambda e: e.tensor_copy(out=idr_t[:], in_=idf[0:64, 0:64]), reads=[b_idf], writes=[b_idr])
        identr = idr_t[:]
        ones_f, b_ones = C.sb([64, 128], F32)
        P.op("pool", lambda e: e.memset(ones_f[:], 1.0), writes=[b_ones])
        VI = [C.sb([64, 64], F32) for _ in range(2)]
        NI = [C.sb([64, 64], F32) for _ in range(2)]
        NS = [C.sb([64, 64], F32) for _ in range(2)]
        for d in range(2):
            t, b = VI[d]
            P.op("pool", lambda e, t=t: e.memset(t[:], 1.0), writes=[b])
            cm, pc = (-1, 1) if d == 0 else (1, -1)
            P.op("pool", lambda e, t=t, cm=cm, pc=pc: e.affine_select(
                out=t[:], in_=t[:], pattern=[[pc, 64]], compare_op=ALU.is_ge, fill=0.0, base=0,
                channel_multiplier=cm), reads=[b], writes=[b])
            n, bn = NI[d]
            P.op("dve", lambda e, t=t, n=n: e.tensor_scalar(out=n[:], in0=t[:], scalar1=-1.0, scalar2=NEGBIG,
                                                            op0=ALU.add, op1=ALU.mult), reads=[b], writes=[bn])
        for d in range(2):
            src, bsrc = VI[1 - d]
            n, bn = NS[d]
            P.op("dve", lambda e, src=src, n=n: e.tensor_tensor(out=n[:], in0=src[:], in1=identf, op=ALU.subtract),
                 reads=[bsrc, b_idf], writes=[bn])
            P.op("dve", lambda e, n=n: e.tensor_scalar(out=n[:], in0=n[:], scalar1=-1.0, scalar2=NEGBIG,
                                                       op0=ALU.add, op1=ALU.mult), reads=[bn], writes=[bn])
        Gs, b_Gs = C.sb([64, 128, 12], F32)
        P.dma("sp", lambda e: e.dma_start(out=Gs[:], in_=gts_d.rearrange("(c p) g -> p c g", p=64)), writes=[b_Gs])

        PB = [C.ps([128, 512], F32) for _ in range(8)]
        pbi = [0]

        def pb():
            t = PB[pbi[0] % 8]
            pbi[0] += 1
            return t

        chains = []
        for d in range(2):
            ch = {"kind": "m", "d": d}
            ch["CC"], ch["bCC"] = C.sb([64, 128], F32)
            ch["BC"], ch["bBC"] = C.sb([64, 128], F32)
            pcc, b_pcc = pb()
            P.op("pe", lambda e, d=d, pcc=pcc: e.matmul(pcc[0:64, 0:128], lhsT=VI[d][0][:], rhs=Gs[:, :, 2 + d],
                                                        start=True, stop=True),
                 reads=[VI[d][1], b_Gs], writes=[b_pcc])
            P.op("dve", lambda e, d=d, pcc=pcc, ch=ch: e.tensor_tensor(out=ch["BC"][:], in0=Gs[:, :, d],
                                                                       in1=pcc[0:64, 0:128], op=ALU.subtract),
                 reads=[b_Gs, b_pcc], writes=[ch["bBC"]])
            chains.append(ch)
        for d in range(2):
            for j in range(2):
                ch = {"kind": "g", "d": d, "j": j, "colg": 8 + d * 2 + j, "colb": 4 + d * 2 + j}
                ch["CC"], ch["bCC"] = C.sb([64, 128], F32)
                ch["NCC"], ch["bNCC"] = C.sb([64, 128], F32)
                ch["BEG"], ch["bBEG"] = C.sb([64, 128], F32)
                pcc, b_pcc = pb()
                P.op("pe", lambda e, d=d, pcc=pcc, ch=ch: e.matmul(pcc[0:64, 0:128], lhsT=VI[d][0][:],
                                                                   rhs=Gs[:, :, ch["colg"]], start=True, stop=True),
                     reads=[VI[d][1], b_Gs], writes=[b_pcc])
                P.op("act", lambda e, pcc=pcc, ch=ch: e.activation(out=ch["CC"][:], in_=pcc[0:64, 0:128], func=AF.Copy),
                     reads=[b_pcc], writes=[ch["bCC"]])
                P.op("dve", lambda e, pcc=pcc, ch=ch: e.tensor_scalar(out=ch["NCC"][:], in0=pcc[0:64, 0:128],
                                                                      scalar1=-1.0, scalar2=None, op0=ALU.mult),
                     reads=[b_pcc], writes=[ch["bNCC"]])
                P.op("act", lambda e, ch=ch: e.activation(out=ch["BEG"][:], in_=ch["CC"][:], func=AF.Exp),
                     reads=[ch["bCC"]], writes=[ch["bBEG"]])
                P.op("dve", lambda e, ch=ch: e.tensor_tensor(out=ch["BEG"][:], in0=ch["BEG"][:],
                                                             in1=Gs[:, :, ch["colb"]], op=ALU.mult),
                     reads=[ch["bBEG"], b_Gs], writes=[ch["bBEG"]])
                chains.append(ch)

        NSLOT = int(os.environ.get("NSLOT", "3"))
        LOOK = int(os.environ.get("LOOK", "2"))
        GC = 4
        NGB = int(os.environ.get("NGB", "3"))

        def W(dst, name, shape, dt):
            t, b = C.sb(shape, dt)
            dst[name] = t
            dst["b_" + name] = b

        for ch in chains:
            ch["slots"] = []
            if ch["kind"] == "m":
                W(ch, "Caug", [128, 2, 257], F32)
                W(ch, "Cb", [128, 2, 257], BF16)
                P.op("pool", lambda e, ch=ch: e.memset(ch["Caug"][:], 0.0), writes=[ch["b_Caug"]])
                P.op("pool", lambda e, ch=ch: e.memset(ch["Cb"][:], 0.0), writes=[ch["b_Cb"]])
                W(ch, "den", [64, 2], F32)
                ch["ho"] = [C.sb([64, 256], F32) for _ in range(3)]
            else:
                W(ch, "S", [128, 128], F32)
                W(ch, "Sb", [128, 128], BF16)
                P.op("pool", lambda e, ch=ch: e.memset(ch["S"][:], 0.0), writes=[ch["b_S"]])
                P.op("pool", lambda e, ch=ch: e.memset(ch["Sb"][:], 0.0), writes=[ch["b_Sb"]])
                W(ch, "vn", [64, 128], BF16)
                ch["og"] = [C.sb([64, 128], F32) for _ in range(3)]
            for s_ in range(NSLOT):
                sl = {}
                W(sl, "rc", [64, 64], F32)
                W(sl, "cum", [128, 64], F32)
                W(sl, "ecb", [128, 64], F32)
                W(sl, "E", [64, 64], F32)
                if ch["kind"] == "m":
                    W(sl, "qs", [128, 2, 64], BF16)
                    W(sl, "DT", [64, 64], F32)
                    W(sl, "PT", [64, 64], BF16)
                    W(sl, "wcol", [64, 1], F32)
                    W(sl, "kw", [64, 256], BF16)
                else:
                    W(sl, "qg", [128, 64], BF16)
                    W(sl, "Dl", [64, 64], F32)
                    W(sl, "AT", [64, 64], BF16)
                    W(sl, "E2", [64, 64], F32)
                    W(sl, "Dn", [64, 64], F32)
                    for nm in ("A0", "A1", "B0", "B1", "X"):
                        W(sl, nm, [64, 64], F32R)
                    W(sl, "Tt", [64, 64], BF16)
                    W(sl, "bv", [64, 128], BF16)
                    W(sl, "kbg", [64, 128], BF16)
                    W(sl, "kd", [64, 128], BF16)
                    W(sl, "dcol", [64, 1], F32)
                    W(sl, "u", [64, 128], F32)
                    W(sl, "wT", [128, 64], BF16)
                ch["slots"].append(sl)
            ch["nout"] = 0

        GT_ = GC * 64
        data = {}
        for d in range(2):
            for bf in range(NGB):
                dd = {}
                dd["mqT"] = C.sb([128, 2, GT_], BF16)
                dd["mkT"] = C.sb([128, 2, GT_], BF16)
                dd["mk"] = C.sb([64, GC, 256], BF16)
                dd["mv"] = C.sb([64, GC, 257], BF16)
                P.op("pool", lambda e, t=dd["mv"][0]: e.memset(t[:, :, 256:257], 1.0), writes=[dd["mv"][1]])
                for j in range(2):
                    dd["gqT%d" % j] = C.sb([128, GT_], BF16)
                    dd["gkT%d" % j] = C.sb([128, GT_], BF16)
                    dd["gk%d" % j] = C.sb([64, GC, 128], BF16)
                    dd["gv%d" % j] = C.sb([64, GC, 128], BF16)
                data[(d, bf)] = dd

        def load_group(d, g):
            dd = data[(d, g % NGB)]
            t0 = g * GT_
            ld = lambda key, fn: P.dma("sp", fn, writes=[dd[key][1]])
            ld("mqT", lambda e: e.dma_start(out=dd["mqT"][0][:],
                                            in_=qta_d[:, t0:t0 + GT_].rearrange("(h p) t -> p h t", p=128)))
            ld("mkT", lambda e: e.dma_start(out=dd["mkT"][0][:],
                                            in_=kta_d[:, t0:t0 + GT_].rearrange("(h p) t -> p h t", p=128)))
            ld("mk", lambda e: e.dma_start(out=dd["mk"][0][:],
                                           in_=ka_d[t0:t0 + GT_, :].rearrange("(c p) x -> p c x", p=64)))
            ld("mv", lambda e: e.dma_start(out=dd["mv"][0][:, :, 0:256],
                                           in_=va_d[t0:t0 + GT_, :].rearrange("(c p) x -> p c x", p=64)))
            for j in range(2):
                ld("gqT%d" % j, lambda e, j=j: e.dma_start(out=dd["gqT%d" % j][0][:],
                                                           in_=qtb_d[j * 128:(j + 1) * 128, t0:t0 + GT_]))
                ld("gkT%d" % j, lambda e, j=j: e.dma_start(out=dd["gkT%d" % j][0][:],
                                                           in_=ktb_d[j * 128:(j + 1) * 128, t0:t0 + GT_]))
                ld("gk%d" % j, lambda e, j=j: e.dma_start(
                    out=dd["gk%d" % j][0][:],
                    in_=kb_d[t0:t0 + GT_, j * 128:(j + 1) * 128].rearrange("(c p) x -> p c x", p=64)))
                ld("gv%d" % j, lambda e, j=j: e.dma_start(
                    out=dd["gv%d" % j][0][:],
                    in_=vb_d[t0:t0 + GT_, j * 128:(j + 1) * 128].rearrange("(c p) x -> p c x", p=64)))

        def cum_common(ch, sl, c, gcol):
            d = ch["d"]
            P.op("pool", lambda e: e.tensor_scalar(out=sl["rc"][:], in0=VI[d][0][:], scalar1=Gs[:, c, gcol:gcol + 1],
                                                   scalar2=None, op0=ALU.mult),
                 reads=[VI[d][1], b_Gs], writes=[sl["b_rc"]])
            pk, b_pk = pb()
            P.op("pe", lambda e: e.matmul(pk[:, 0:64], lhsT=ones_f[:], rhs=sl["rc"][:], start=True, stop=True),
                 reads=[b_ones, sl["b_rc"]], writes=[b_pk])
            P.op("act", lambda e: e.activation(out=sl["cum"][:], in_=pk[:, 0:64], func=AF.Copy),
                 reads=[b_pk], writes=[sl["b_cum"]])
            P.op("act", lambda e: e.activation(out=sl["ecb"][:], in_=sl["cum"][:], func=AF.Exp),
                 reads=[sl["b_cum"]], writes=[sl["b_ecb"]])
            P.op("dve", lambda e: e.tensor_tensor(out=sl["E"][:], in0=sl["cum"][0:64, :], in1=NI[d][0][:], op=ALU.add),
                 reads=[sl["b_cum"], NI[d][1]], writes=[sl["b_E"]])

        def dtiles(ch, c):
            d = ch["d"]
            dd = data[(d, (c // GC) % NGB)]
            lc = c % GC
            return dd, lc, slice(lc * 64, lc * 64 + 64), (63 if d == 0 else 0)

        def mlstm_pre(ch, c, sl):
            d = ch["d"]
            dd, lc, ts, last = dtiles(ch, c)
            qT, b_qT = dd["mqT"]
            kT, b_kT = dd["mkT"]
            kk, b_kk = dd["mk"]
            cum_common(ch, sl, c, 2 + d)
            yield
            for h in range(2):
                P.op("dve", lambda e, h=h: e.tensor_tensor(out=sl["qs"][:, h, :], in0=qT[:, h, ts], in1=sl["ecb"][:],
                                                            op=ALU.mult),
                     reads=[b_qT, sl["b_ecb"]], writes=[sl["b_qs"]])
            P.op("act", lambda e: e.activation(out=sl["DT"][:], in_=sl["E"][:], func=AF.Exp, bias=ch["BC"][:, c:c + 1]),
                 reads=[sl["b_E"], ch["bBC"]], writes=[sl["b_DT"]])
            P.op("act", lambda e: e.activation(out=sl["wcol"][:], in_=sl["cum"][0:64, last:last + 1], func=AF.Exp,
                                               bias=ch["BC"][:, c:c + 1]),
                 reads=[sl["b_cum"], ch["bBC"]], writes=[sl["b_wcol"]])
            yield
            pst, b_pst = pb()
            P.op("pe", [(lambda e, h=h: e.matmul(pst[0:64, 0:64], lhsT=kT[:, h, ts], rhs=qT[:, h, ts],
                                                 start=(h == 0), stop=(h == 1))) for h in range(2)],
                 reads=[b_kT, b_qT], writes=[b_pst])
            P.op("act", lambda e: e.activation(out=sl["kw"][:], in_=kk[:, lc, :], func=AF.Copy, scale=sl["wcol"][:, 0:1]),
                 reads=[b_kk, sl["b_wcol"]], writes=[sl["b_kw"]])
            P.op("dve", lambda e: e.tensor_tensor(out=sl["PT"][:], in0=pst[0:64, 0:64], in1=sl["DT"][:], op=ALU.mult),
                 reads=[b_pst, sl["b_DT"]], writes=[sl["b_PT"]])
            yield

        def mlstm_seq(ch, c, sl):
            d = ch["d"]
            dd, lc, ts, last = dtiles(ch, c)
            vv, b_vv = dd["mv"]
            pn, b_pn = pb()
            P.op("pe", [lambda e: e.matmul(pn[0:64, 0:257], lhsT=sl["qs"][:, 0, :], rhs=ch["Cb"][:, 0, :],
                                           start=True, stop=False),
                        lambda e: e.matmul(pn[0:64, 0:257], lhsT=sl["qs"][:, 1, :], rhs=ch["Cb"][:, 1, :],
                                           start=False, stop=False),
                        lambda e: e.matmul(pn[0:64, 0:257], lhsT=sl["PT"][:], rhs=vv[:, lc, :],
                                           start=False, stop=True)],
                 reads=[sl["b_qs"], ch["b_Cb"], sl["b_PT"], b_vv], writes=[b_pn])
            pu = [pb(), pb()]
            for h in range(2):
                P.op("pe", lambda e, h=h: e.matmul(pu[h][0][:, 0:257], lhsT=sl["kw"][:, h * 128:(h + 1) * 128],
                                                   rhs=vv[:, lc, :], start=True, stop=True),
                     reads=[sl["b_kw"], b_vv], writes=[pu[h][1]])
            for h in range(2):
                P.op("dve", lambda e, h=h: e.scalar_tensor_tensor(
                    out=ch["Caug"][:, h, :], in0=ch["Caug"][:, h, :], scalar=sl["ecb"][:, last:last + 1],
                    in1=pu[h][0][:, 0:257], op0=ALU.mult, op1=ALU.add),
                    reads=[sl["b_ecb"], pu[h][1]], writes=[ch["b_Caug"]])
            P.op("act", lambda e: e.activation(out=ch["den"][:, 0:1], in_=pn[0:64, 256:257], func=AF.Abs),
                 reads=[b_pn], writes=[ch["b_den"]])
            P.op("act", lambda e: e.activation(out=ch["Cb"][:], in_=ch["Caug"][:], func=AF.Copy),
                 reads=[ch["b_Caug"]], writes=[ch["b_Cb"]])
            P.op("dve", lambda e: e.tensor_scalar(out=ch["den"][:, 0:1], in0=ch["den"][:, 0:1], scalar1=1.0,
                                                  scalar2=None, op0=ALU.max), reads=[ch["b_den"]], writes=[ch["b_den"]])
            P.op("dve", lambda e: e.reciprocal(out=ch["den"][:, 1:2], in_=ch["den"][:, 0:1]),
                 reads=[ch["b_den"]], writes=[ch["b_den"]])
            ho, b_ho = ch["ho"][ch["nout"] % 3]
            ch["nout"] += 1
            P.op("act", lambda e: e.activation(out=ho[:], in_=pn[0:64, 0:256], func=AF.Copy, scale=ch["den"][:, 1:2]),
                 reads=[b_pn, ch["b_den"]], writes=[b_ho])
            ob = Buf()
            outbufs.append(ob)
            P.dma("sp", lambda e: e.dma_start(out=hab_d[c * 64:(c + 1) * 64, d * 256:(d + 1) * 256], in_=ho[:]),
                  reads=[b_ho], writes=[ob])
            yield

        def gdn_pre(ch, c, sl):
            d, j = ch["d"], ch["j"]
            dd, lc, ts, last = dtiles(ch, c)
            qT, b_qT = dd["gqT%d" % j]
            kT, b_kT = dd["gkT%d" % j]
            kk, b_kk = dd["gk%d" % j]
            vv, b_vv = dd["gv%d" % j]
            beta = Gs[:, c, ch["colb"]:ch["colb"] + 1]
            cum_common(ch, sl, c, ch["colg"])
            P.op("dve", lambda e: e.tensor_scalar(out=sl["bv"][:], in0=vv[:, lc, :], scalar1=beta, scalar2=None,
                                                  op0=ALU.mult), reads=[b_vv, b_Gs], writes=[sl["b_bv"]])
            P.op("act", lambda e: e.activation(out=sl["kbg"][:], in_=kk[:, lc, :], func=AF.Copy, scale=ch["BEG"][:, c:c + 1]),
                 reads=[b_kk, ch["bBEG"]], writes=[sl["b_kbg"]])
            yield
            P.op("dve", lambda e: e.tensor_tensor(out=sl["qg"][:], in0=qT[:, ts], in1=sl["ecb"][:], op=ALU.mult),
                 reads=[b_qT, sl["b_ecb"]], writes=[sl["b_qg"]])
            P.op("act", lambda e: e.activation(out=sl["Dl"][:], in_=sl["E"][:], func=AF.Exp, bias=ch["NCC"][:, c:c + 1]),
                 reads=[sl["b_E"], ch["bNCC"]], writes=[sl["b_Dl"]])
            P.op("dve", lambda e: e.scalar_tensor_tensor(out=sl["E2"][:], in0=sl["cum"][0:64, :], scalar=-1.0,
                                                         in1=NS[d][0][:], op0=ALU.mult, op1=ALU.add),
                 reads=[sl["b_cum"], NS[d][1]], writes=[sl["b_E2"]])
            P.op("act", lambda e: e.activation(out=sl["Dn"][:], in_=sl["E2"][:], func=AF.Exp, bias=ch["CC"][:, c:c + 1]),
                 reads=[sl["b_E2"], ch["bCC"]], writes=[sl["b_Dn"]])
            P.op("act", lambda e: e.activation(out=sl["dcol"][:], in_=sl["cum"][0:64, last:last + 1], func=AF.Exp,
                                               bias=ch["NCC"][:, c:c + 1]),
                 reads=[sl["b_cum"], ch["bNCC"]], writes=[sl["b_dcol"]])
            yield
            pkq, b_pkq = pb()
            P.op("pe", [lambda e: e.matmul(pkq[0:64, 0:64], lhsT=kT[:, ts], rhs=kT[:, ts], start=True, stop=True),
                        lambda e: e.matmul(pkq[0:64, 64:128], lhsT=kT[:, ts], rhs=qT[:, ts], start=True, stop=True)],
                 reads=[b_kT, b_qT], writes=[b_pkq])
            P.op("dve", lambda e: e.tensor_scalar(out=sl["kd"][:], in0=kk[:, lc, :], scalar1=sl["dcol"][:, 0:1],
                                                  scalar2=None, op0=ALU.mult),
                 reads=[b_kk, sl["b_dcol"]], writes=[sl["b_kd"]])
            P.op("dve", lambda e: e.tensor_tensor(out=sl["AT"][:], in0=pkq[0:64, 64:128], in1=sl["Dl"][:], op=ALU.mult),
                 reads=[b_pkq, sl["b_Dl"]], writes=[sl["b_AT"]])
            P.op("dve", lambda e: e.scalar_tensor_tensor(out=sl["A0"][:], in0=pkq[0:64, 0:64], scalar=beta,
                                                         in1=sl["Dn"][:], op0=ALU.mult, op1=ALU.mult),
                 reads=[b_pkq, b_Gs, sl["b_Dn"]], writes=[sl["b_A0"]])
            yield
            pB, b_pB = pb()
            P.op("pe", lambda e: e.matmul(pB[0:64, 0:64], lhsT=sl["A0"][:], rhs=identr, start=True, stop=True),
                 reads=[sl["b_A0"], b_idr], writes=[b_pB])
            P.op("act", lambda e: e.activation(out=sl["B0"][:], in_=pB[0:64, 0:64], func=AF.Copy),
                 reads=[b_pB], writes=[sl["b_B0"]])
            P.op("dve", lambda e: e.tensor_tensor(out=sl["X"][:], in0=identf, in1=pB[0:64, 0:64], op=ALU.subtract),
                 reads=[b_idf, b_pB], writes=[sl["b_X"]])
            yield
            cur = 0
            for lvl in range(5):
                A, bA = sl["A%d" % cur], sl["b_A%d" % cur]
                B, bB = sl["B%d" % cur], sl["b_B%d" % cur]
                An, bAn = sl["A%d" % (1 - cur)], sl["b_A%d" % (1 - cur)]
                Bn, bBn = sl["B%d" % (1 - cur)], sl["b_B%d" % (1 - cur)]
                pA, b_pA = pb()
                P.op("pe", lambda e, A=A, B=B, pA=pA: e.matmul(pA[0:64, 0:64], lhsT=FR(B[:]), rhs=FR(A[:]), start=True, stop=True),
                     reads=[bA, bB], writes=[b_pA])
                if lvl < 4:
                    pBn, b_pBn = pb()
                    P.op("pe", lambda e, A=A, B=B, pBn=pBn: e.matmul(pBn[0:64, 0:64], lhsT=FR(A[:]), rhs=FR(B[:]),
                                                                     start=True, stop=True),
                         reads=[bA, bB], writes=[b_pBn])
                P.op("act", lambda e, An=An, pA=pA: e.activation(out=An[:], in_=pA[0:64, 0:64], func=AF.Copy),
                     reads=[b_pA], writes=[bAn])
                if lvl < 4:
                    P.op("dve", lambda e, Bn=Bn, pBn=pBn: e.tensor_copy(out=Bn[:], in_=pBn[0:64, 0:64]),
                         reads=[b_pBn], writes=[bBn])
                yield
                pX, b_pX = pb()
                P.op("pe", lambda e, An=An, pX=pX: e.matmul(pX[0:64, 0:64], lhsT=FR(An[:]), rhs=FR(sl["X"][:]),
                                                            start=True, stop=True),
                     reads=[bAn, sl["b_X"]], writes=[b_pX])
                if lvl < 4:
                    P.op("dve", lambda e, pX=pX: e.tensor_tensor(out=sl["X"][:], in0=pX[0:64, 0:64], in1=F(sl["X"][:]),
                                                                 op=ALU.add),
                         reads=[b_pX, sl["b_X"]], writes=[sl["b_X"]])
                else:
                    P.op("dve", lambda e, pX=pX: e.tensor_tensor(out=sl["Tt"][:], in0=pX[0:64, 0:64], in1=F(sl["X"][:]),
                                                                 op=ALU.add),
                         reads=[b_pX, sl["b_X"]], writes=[sl["b_Tt"]])
                cur = 1 - cur
                yield
            pu, b_pu = pb()
            P.op("pe", lambda e: e.matmul(pu[0:64, 0:128], lhsT=sl["Tt"][:], rhs=sl["bv"][:], start=True, stop=True),
                 reads=[sl["b_Tt"], sl["b_bv"]], writes=[b_pu])
            pw, b_pw = pb()
            P.op("pe", lambda e: e.matmul(pw[:, 0:64], lhsT=sl["kbg"][:], rhs=sl["Tt"][:], start=True, stop=True),
                 reads=[sl["b_kbg"], sl["b_Tt"]], writes=[b_pw])
            P.op("act", lambda e: e.activation(out=sl["u"][:], in_=pu[0:64, 0:128], func=AF.Copy),
                 reads=[b_pu], writes=[sl["b_u"]])
            P.op("dve", lambda e: e.tensor_copy(out=sl["wT"][:], in_=pw[:, 0:64]), reads=[b_pw], writes=[sl["b_wT"]])
            yield

        def gdn_seq(ch, c, sl):
            d, j = ch["d"], ch["j"]
            dd, lc, ts, last = dtiles(ch, c)
            pws, b_pws = pb()
            P.op("pe", lambda e: e.matmul(pws[0:64, 0:128], lhsT=sl["wT"][:], rhs=ch["Sb"][:], start=True, stop=True),
                 reads=[sl["b_wT"], ch["b_Sb"]], writes=[b_pws])
            P.op("dve", lambda e: e.tensor_tensor(out=ch["vn"][:], in0=sl["u"][:], in1=pws[0:64, 0:128], op=ALU.subtract),
                 reads=[sl["b_u"], b_pws], writes=[ch["b_vn"]])
            yield
            pup, b_pup = pb()
            P.op("pe", lambda e: e.matmul(pup[:, 0:128], lhsT=sl["kd"][:], rhs=ch["vn"][:], start=True, stop=True),
                 reads=[sl["b_kd"], ch["b_vn"]], writes=[b_pup])
            po, b_po = pb()
            P.op("pe", [lambda e: e.matmul(po[0:64, 0:128], lhsT=sl["qg"][:], rhs=ch["Sb"][:], start=True, stop=False),
                        lambda e: e.matmul(po[0:64, 0:128], lhsT=sl["AT"][:], rhs=ch["vn"][:], start=False, stop=True)],
                 reads=[sl["b_qg"], ch["b_Sb"], sl["b_AT"], ch["b_vn"]], writes=[b_po])
            P.op("dve", lambda e: e.scalar_tensor_tensor(out=ch["S"][:], in0=ch["S"][:], scalar=sl["ecb"][:, last:last + 1],
                                                         in1=pup[:, 0:128], op0=ALU.mult, op1=ALU.add),
                 reads=[sl["b_ecb"], b_pup], writes=[ch["b_S"]])
            og, b_og = ch["og"][ch["nout"] % 3]
            ch["nout"] += 1
            P.op("act", lambda e: e.activation(out=og[:], in_=po[0:64, 0:128], func=AF.Copy), reads=[b_po], writes=[b_og])
            P.op("act", lambda e: e.activation(out=ch["Sb"][:], in_=ch["S"][:], func=AF.Copy),
                 reads=[ch["b_S"]], writes=[ch["b_Sb"]])
            ob = Buf()
            outbufs.append(ob)
            co = 512 + d * 256 + j * 128
            P.dma("sp", lambda e: e.dma_start(out=hab_d[c * 64:(c + 1) * 64, co:co + 128], in_=og[:]),
                  reads=[b_og], writes=[ob])
            yield

        ngr = (nchunk + GC - 1) // GC
        NG_ALL = 128 // GC
        loaded = set()

        def ensure_group(gi):
            if gi < ngr and gi not in loaded:
                loaded.add(gi)
                load_group(0, gi)
                load_group(1, NG_ALL - 1 - gi)

        def chunk_of(ch, i):
            return i if ch["d"] == 0 else 127 - i

        def run_round_robin(gens):
            alive = list(gens)
            while alive:
                nxt = []
                for g_ in alive:
                    try:
                        next(g_)
                        nxt.append(g_)
                    except StopIteration:
                        pass
                alive = nxt

        def pre_gens(i):
            ensure_group(i // GC)
            ensure_group(i // GC + 1)
            out = []
            for ch in chains:
                c = chunk_of(ch, i)
                sl = ch["slots"][i % NSLOT]
                out.append(mlstm_pre(ch, c, sl) if ch["kind"] == "m" else gdn_pre(ch, c, sl))
            return out

        def seq_gens(i):
            out = []
            for ch in chains:
                c = chunk_of(ch, i)
                sl = ch["slots"][i % NSLOT]
                out.append(mlstm_seq(ch, c, sl) if ch["kind"] == "m" else gdn_seq(ch, c, sl))
            return out

        for i in range(min(LOOK, nchunk)):
            run_round_robin(pre_gens(i))
        for i in range(nchunk):
            gens = seq_gens(i)
            if i + LOOK < nchunk:
                gens = gens + pre_gens(i + LOOK)
            run_round_robin(gens)
        P.wait_all("sp", outbufs)
        P.emit()
    return nc


TOK = T // NCORE
NB = TOK // 128
D_FF = 5632
NFF = D_FF // 128


def build_C():
    nc = bass.Bass("TRN2", target_bir_lowering=False)
    din = lambda name, shape: nc.dram_tensor(name, shape, F32, kind="ExternalInput").ap()
    x_d = din("x", [TOK, D])
    modin_d = din("modfm", [128, 96])
    nmix_d = din("norm_mix_w", [D])
    nffn_d = din("norm_ffn_w", [D])
    nfin_d = din("norm_final_w", [D])
    nwa_d = din("mlstm_norm_w", [D])
    nwb_d = din("gdn_norm_w", [D])
    w3_d = din("w3", [D, 4 * D])
    wa_d = din("w_branch_a", [D, D])
    wb_d = din("w_branch_b", [D, D])
    wo_d = din("w_out", [D, D])
    wgu_d = din("w_gate_up", [D, 2 * D_FF])
    wd_d = din("w_down", [D_FF, D])
    hsrc = [din(nm, [TOK, D]) for nm in ("haf", "hab", "obf", "obb")]
    out_d = nc.dram_tensor("out", [TOK, D], F32, kind="ExternalOutput").ap()
    x1_d = nc.dram_tensor("x1_scr", [TOK, D], F32, kind="Internal").ap()
    x2_d = nc.dram_tensor("x2_scr", [TOK, D], F32, kind="Internal").ap()
    b_x1d = [Buf() for _ in range(NB)]
    b_x2d = [[Buf() for _ in range(8)] for _ in range(NB)]
    outbufs = []

    with ExitStack() as es0:
        sems = [es0.enter_context(nc.semaphore(f"s{i}")) for i in range(100)]
        P = Prog(nc, sems)
        C0 = Ctx(nc, es0)
        mod, b_mod = C0.sb([128, 96], F32, "mod")
        scm, b_scm = C0.sb([128, KC], F32, "scm")
        scf, b_scf = C0.sb([128, KC], F32, "scf")
        gm_row, b_gm = C0.sb([128, D], F32, "gm_row")
        gf_row, b_gf = C0.sb([128, D], F32, "gf_row")
        ident, b_id, idf, b_idf = make_identity(P, C0)
        with ExitStack() as es1:
            C = Ctx(nc, es1)
            nw, b_nw = C.sb([128, KC], F32)
            nf, b_nf = C.sb([128, KC], F32)
            P.dma("sp", lambda e: e.dma_start(out=nw[:], in_=nmix_d.rearrange("(k p) -> p k", p=128),
                                              allow_slow_non_contiguous=True), writes=[b_nw])
            P.dma("sp", lambda e: e.dma_start(out=nf[:], in_=nffn_d.rearrange("(k p) -> p k", p=128),
                                              allow_slow_non_contiguous=True), writes=[b_nf])
            P.dma("sp", lambda e: e.dma_start(out=mod[:], in_=modin_d), writes=[b_mod])
            P.op("dve", lambda e: e.scalar_tensor_tensor(out=scm[:], in0=mod[:, 16:32], scalar=1.0, in1=nw[:],
                                                         op0=ALU.add, op1=ALU.mult), reads=[b_mod, b_nw], writes=[b_scm])
            P.op("dve", lambda e: e.scalar_tensor_tensor(out=scf[:], in0=mod[:, 64:80], scalar=1.0, in1=nf[:],
                                                         op0=ALU.add, op1=ALU.mult), reads=[b_mod, b_nf], writes=[b_scf])
            ones_f, b_ones = C.sb([128, 128], F32)
            P.op("pool", lambda e: e.memset(ones_f[:], 1.0), writes=[b_ones])
            dg, b_dg = C.sb([128, 128], F32)
            pg, b_pg = C.ps([128, 128], F32)
            for (row, b_row, off) in ((gm_row, b_gm, 32), (gf_row, b_gf, 80)):
                for k in range(KC):
                    P.op("dve", lambda e, k=k, off=off: e.tensor_scalar(out=dg[:], in0=idf[:], scalar1=mod[:, off + k:off + k + 1],
                                                                        scalar2=None, op0=ALU.mult),
                         reads=[b_idf, b_mod], writes=[b_dg])
                    P.op("pe", lambda e: e.matmul(pg[:], lhsT=ones_f[:], rhs=dg[:], start=True, stop=True),
                         reads=[b_ones, b_dg], writes=[b_pg])
                    P.op("act", lambda e, k=k, row=row: e.activation(out=row[:, k * 128:(k + 1) * 128], in_=pg[:], func=AF.Copy),
                         reads=[b_pg], writes=[b_row])
            P.emit()

        def gemm_tok(C, specs, ncb, cbw, epilogue, psums):
            wbufs = []
            for (aT, b_aT, wfn, kch) in specs:
                wbufs.append([C.sb([128, kch, cbw], BF16) for _ in range(2)])
            for cb in range(ncb):
                for si, (aT, b_aT, wfn, kch) in enumerate(specs):
                    wt, b_wt = wbufs[si][cb % 2]
                    src = wfn(cb)
                    P.dma("pool", lambda e, wt=wt, src=src: e.dma_start(
                        out=wt[:], in_=src.rearrange("(k p) n -> p k n", p=128)), writes=[b_wt])
                for b in range(NB):
                    outs = []
                    for si, (aT, b_aT, wfn, kch) in enumerate(specs):
                        wt, b_wt = wbufs[si][cb % 2]
                        pp, b_pp = psums[si][(cb * NB + b) % 2]
                        P.op("pe", [(lambda e, k=k, aT=aT, wt=wt, pp=pp, b=b, kch=kch: e.matmul(
                            pp[:, 0:cbw], lhsT=aT[:, k, b * 128:(b + 1) * 128], rhs=wt[:, k, :],
                            start=(k == 0), stop=(k == kch - 1))) for k in range(kch)],
                            reads=[b_aT, b_wt], writes=[b_pp])
                        outs.append((pp, b_pp))
                    epilogue(b, cb, outs)

        def transpose_blocks(src, b_src, dst, b_dst, pT, b_pT, copy_eng="act"):
            for b in range(NB):
                P.op("pe", [(lambda e, k=k, b=b: e.transpose(out=pT[:, k, :], in_=src[:, b, k * 128:(k + 1) * 128],
                                                             identity=ident[:])) for k in range(KC)],
                     reads=[b_src, b_id], writes=[b_pT])
                if copy_eng == "act":
                    P.op("act", lambda e, b=b: e.activation(out=dst[:, :, b * 128:(b + 1) * 128], in_=pT[:], func=AF.Copy),
                         reads=[b_pT], writes=[b_dst])
                else:
                    P.op("dve", lambda e, b=b: e.tensor_copy(out=dst[:, :, b * 128:(b + 1) * 128], in_=pT[:]),
                         reads=[b_pT], writes=[b_dst])

        def norm_to_T(C, src_fn, sc_t, b_sc, sh_ap_fn, b_sh, dstT, b_dstT, pT, b_pT):
            junk, b_junk = C.sb([128, D], BF16)
            st, b_st = C.sb([128, NB, 3], F32)
            xs2 = [C.sb([128, D], BF16) for _ in range(2)]
            pre = {0: src_fn(0)}
            for b in range(NB):
                if b + 1 < NB:
                    pre[b + 1] = src_fn(b + 1)
                xt, b_xt = pre.pop(b)
                xs, b_xs = xs2[b % 2]
                P.op("act", lambda e, b=b, xt=xt: e.activation(out=junk[:], in_=xt, func=AF.Square, accum_out=st[:, b, 0:1]),
                     reads=[b_xt], writes=[b_junk, b_st])
                P.op("act", lambda e, b=b: e.activation(out=st[:, b, 1:2], in_=st[:, b, 0:1], func=AF.Sqrt, scale=1.0 / D,
                                                        bias=EPS), reads=[b_st], writes=[b_st])
                P.op("dve", lambda e, b=b: e.reciprocal(out=st[:, b, 2:3], in_=st[:, b, 1:2]), reads=[b_st], writes=[b_st])
                P.op("dve", lambda e, b=b, xt=xt, xs=xs: e.tensor_scalar(out=xs[:], in0=xt, scalar1=st[:, b, 2:3], scalar2=None,
                                                                         op0=ALU.mult), reads=[b_st, b_xt], writes=[b_xs])
                P.op("pe", [(lambda e, k=k, xs=xs: e.transpose(out=pT[:, k, :], in_=xs[:, k * 128:(k + 1) * 128],
                                                               identity=ident[:])) for k in range(KC)],
                     reads=[b_xs, b_id], writes=[b_pT])
                P.op("act", [(lambda e, k=k, b=b: e.activation(out=dstT[:, k, b * 128:(b + 1) * 128], in_=pT[:, k, :],
                                                               func=AF.Identity, scale=sc_t[:, k:k + 1], bias=sh_ap_fn(k)))
                             for k in range(KC)], reads=[b_pT, b_sc, b_sh], writes=[b_dstT])

        CBW = 256
        NCB = D // CBW
        with ExitStack() as es2:
            C = Ctx(nc, es2)
            pT, b_pT = C.ps([128, KC, 128], BF16)
            psA = [C.ps([128, 512], F32) for _ in range(2)]
            psB = [C.ps([128, 512], F32) for _ in range(2)]
            GT, b_GT = C.sb([128, KC, TOK], BF16, "GT")
            with ExitStack() as es2m:
                Cm = Ctx(nc, es2m)
                hT, b_hT = Cm.sb([128, KC, TOK], BF16, "hT")
                Gt, b_Gt = Cm.sb([128, NB, D], BF16, "Gt")
                Mt, b_Mt = Cm.sb([128, NB, D], BF16, "Mt")
                nrow = [Cm.sb([128, D], F32) for _ in range(2)]
                P.dma("sp", lambda e: e.dma_start(out=nrow[0][0][:], in_=nwa_d.partition_broadcast(128)), writes=[nrow[0][1]])
                P.dma("sp", lambda e: e.dma_start(out=nrow[1][0][:], in_=nwb_d.partition_broadcast(128)), writes=[nrow[1][1]])
                with ExitStack() as es2a:
                    Ca = Ctx(nc, es2a)
                    xb = [Ca.sb([128, D], F32) for _ in range(2)]

                    def src_x(b):
                        xt, b_xt = xb[b % 2]
                        P.dma("sp", lambda e: e.dma_start(out=xt[:], in_=x_d[b * 128:(b + 1) * 128, :]), writes=[b_xt])
                        return xt[:], b_xt
                    norm_to_T(Ca, src_x, scm, b_scm, lambda k: mod[:, k:k + 1], b_mod, hT, b_hT, pT, b_pT)
                    P.emit()
                for br in range(2):
                    H = 4 if br == 0 else 16
                    dv = D // H
                    with ExitStack() as es2h:
                        Ch = Ctx(nc, es2h)
                        hl = [Ch.sb([128, D], F32) for _ in range(2)]
                        hr = [Ch.sb([128, D], F32) for _ in range(2)]
                        jk, b_jk = Ch.sb([128, 512], BF16)
                        ssn, b_ssn = Ch.sb([128, 16, 2], F32)
                        def load_h(b, br=br, hl=hl, hr=hr):
                            a, b_a = hl[b % 2]
                            c2, b_c2 = hr[b % 2]
                            P.dma("sp", lambda e: e.dma_start(out=a[:], in_=hsrc[2 * br][b * 128:(b + 1) * 128, :]),
                                  writes=[b_a])
                            P.dma("sp", lambda e: e.dma_start(out=c2[:], in_=hsrc[2 * br + 1][b * 128:(b + 1) * 128, :]),
                                  writes=[b_c2])
                        load_h(0)
                        for b in range(NB):
                            a, b_a = hl[b % 2]
                            c2, b_c2 = hr[b % 2]
                            if b + 1 < NB:
                                load_h(b + 1)
                            P.op("dve", lambda e, a=a, c2=c2: e.tensor_tensor(out=a[:], in0=a[:], in1=c2[:], op=ALU.add),
                                 reads=[b_a, b_c2], writes=[b_a])
                            P.op("act", [(lambda e, a=a, h=h, dv=dv: e.activation(out=jk[:, 0:dv], in_=a[:, h * dv:(h + 1) * dv],
                                                                                   func=AF.Square, accum_out=ssn[:, h, 0:1]))
                                         for h in range(H)], reads=[b_a], writes=[b_jk, b_ssn])
                            P.op("act", lambda e, H=H, dv=dv: e.activation(out=ssn[:, 0:H, 1:2], in_=ssn[:, 0:H, 0:1], func=AF.Ln,
                                                                           scale=1.0 / dv, bias=EPS), reads=[b_ssn], writes=[b_ssn])
                            P.op("act", lambda e, H=H: e.activation(out=ssn[:, 0:H, 0:1], in_=ssn[:, 0:H, 1:2], func=AF.Exp, scale=-0.5),
                                 reads=[b_ssn], writes=[b_ssn])
                            P.op("act", [(lambda e, a=a, h=h, dv=dv: e.activation(out=a[:, h * dv:(h + 1) * dv],
                                                                                   in_=a[:, h * dv:(h + 1) * dv], func=AF.Copy,
                                                                                   scale=ssn[:, h, 0:1])) for h in range(H)],
                                 reads=[b_ssn, b_a], writes=[b_a])
                            P.op("dve", lambda e, a=a, b=b, br=br: e.tensor_tensor(out=Gt[:, b, :], in0=a[:], in1=nrow[br][0][:],
                                                                                   op=ALU.mult),
                                 reads=[b_a, nrow[br][1]], writes=[b_Gt])
                        P.emit()
                    with ExitStack() as esg:
                        Cg = Ctx(nc, esg)
                        sg = [Cg.sb([128, CBW], F32) for _ in range(2)]

                        def ep_gate(b, cb, outs, br=br, sg=sg):
                            pp, b_pp = outs[0]
                            s_, b_s = sg[(cb * NB + b) % 2]
                            P.op("act", lambda e: e.activation(out=s_[:], in_=pp[:, 0:CBW],
                                                               func=(AF.Sigmoid if br == 0 else AF.Silu)),
                                 reads=[b_pp], writes=[b_s])
                            P.op("dve", lambda e: e.tensor_tensor(out=Gt[:, b, cb * CBW:(cb + 1) * CBW],
                                                                  in0=Gt[:, b, cb * CBW:(cb + 1) * CBW], in1=s_[:], op=ALU.mult),
                                 reads=[b_s, b_Gt], writes=[b_Gt])
                        gemm_tok(Cg, [(hT, b_hT, (lambda cb, br=br: w3_d[:, br * D + cb * CBW: br * D + (cb + 1) * CBW]), KC)],
                                 NCB, CBW, ep_gate, [psA])
                        P.emit()
                    transpose_blocks(Gt, b_Gt, GT, b_GT, pT, b_pT)
                    with ExitStack() as esg:
                        Cg = Ctx(nc, esg)
                        sg = [Cg.sb([128, CBW], F32) for _ in range(2)]

                        def ep_merge(b, cb, outs, br=br, sg=sg):
                            (py, b_py), (pgm, b_pgm) = outs
                            s_, b_s = sg[(cb * NB + b) % 2]
                            P.op("act", lambda e: e.activation(out=s_[:], in_=pgm[:, 0:CBW], func=AF.Sigmoid),
                                 reads=[b_pgm], writes=[b_s])
                            if br == 0:
                                P.op("dve", lambda e: e.tensor_tensor(out=Mt[:, b, cb * CBW:(cb + 1) * CBW], in0=py[:, 0:CBW],
                                                                      in1=s_[:], op=ALU.mult), reads=[b_py, b_s], writes=[b_Mt])
                            else:
                                P.op("dve", lambda e: e.tensor_tensor(out=s_[:], in0=py[:, 0:CBW], in1=s_[:], op=ALU.mult),
                                     reads=[b_py, b_s], writes=[b_s])
                                P.op("dve", lambda e: e.tensor_tensor(out=Mt[:, b, cb * CBW:(cb + 1) * CBW],
                                                                      in0=Mt[:, b, cb * CBW:(cb + 1) * CBW], in1=s_[:], op=ALU.add),
                                     reads=[b_s, b_Mt], writes=[b_Mt])
                        wbr = wa_d if br == 0 else wb_d
                        gemm_tok(Cg, [(GT, b_GT, (lambda cb, wbr=wbr: wbr[:, cb * CBW:(cb + 1) * CBW]), KC),
                                      (hT, b_hT, (lambda cb, br=br: w3_d[:, (2 + br) * D + cb * CBW:(2 + br) * D + (cb + 1) * CBW]), KC)],
                                 NCB, CBW, ep_merge, [psA, psB])
                        P.emit()
                transpose_blocks(Mt, b_Mt, GT, b_GT, pT, b_pT)
                P.emit()
            with ExitStack() as es2c:
                Cc = Ctx(nc, es2c)
                xres, b_xres = Cc.sb([128, NB, D], F32)
                P.dma("sp", lambda e: e.dma_start(out=xres[:], in_=x_d.rearrange("(b p) n -> p b n", p=128)), writes=[b_xres])
                OW = 512
                tmp = [Cc.sb([128, OW], F32) for _ in range(2)]

                def ep_out(b, cb, outs):
                    pp, b_pp = outs[0]
                    t_, b_t = tmp[(cb * NB + b) % 2]
                    P.op("dve", lambda e: e.tensor_tensor(out=t_[:], in0=pp[:, 0:OW], in1=gm_row[:, cb * OW:(cb + 1) * OW],
                                                          op=ALU.mult), reads=[b_pp, b_gm], writes=[b_t])
                    P.op("dve", lambda e: e.tensor_tensor(out=xres[:, b, cb * OW:(cb + 1) * OW],
                                                           in0=xres[:, b, cb * OW:(cb + 1) * OW], in1=t_[:], op=ALU.add),
                         reads=[b_t, b_xres], writes=[b_xres])
                gemm_tok(Cc, [(GT, b_GT, (lambda cb: wo_d[:, cb * OW:(cb + 1) * OW]), KC)], D // OW, OW, ep_out, [psA])
                for b in range(NB):
                    P.dma("sp", lambda e, b=b: e.dma_start(out=x1_d[b * 128:(b + 1) * 128, :], in_=xres[:, b, :]),
                          reads=[b_xres], writes=[b_x1d[b]])
                P.wait_all("sp", b_x1d)
                P.emit()
        with ExitStack() as es4:
            C = Ctx(nc, es4)
            actT, b_actT = C.sb([128, NFF, TOK], BF16, "actT")
            with ExitStack() as es4h:
                Chh = Ctx(nc, es4h)
                hf2, b_hf2 = Chh.sb([128, KC, TOK], BF16, "hf2")
                pT, b_pT = Chh.ps([128, KC, 128], BF16)
                pg_ = [Chh.ps([128, 512], F32) for _ in range(2)]
                pu_ = [Chh.ps([128, 512], F32) for _ in range(2)]
                with ExitStack() as es4a:
                    Ca = Ctx(nc, es4a)
                    xb = [Ca.sb([128, D], F32) for _ in range(2)]

                    def src_x1(b):
                        xt, b_xt = xb[b % 2]
                        P.dma("sp", lambda e: e.dma_start(out=xt[:], in_=x1_d[b * 128:(b + 1) * 128, :]),
                              reads=[b_x1d[b]], writes=[b_xt])
                        return xt[:], b_xt
                    norm_to_T(Ca, src_x1, scf, b_scf, lambda k: mod[:, 48 + k:48 + k + 1], b_mod, hf2, b_hf2, pT, b_pT)
                    P.emit()
                with ExitStack() as es4b:
                    Cb_ = Ctx(nc, es4b)
                    wg = [Cb_.sb([128, KC, 128], BF16) for _ in range(3)]
                    wu = [Cb_.sb([128, KC, 128], BF16) for _ in range(3)]
                    sl_ = [Cb_.sb([128, 512], F32) for _ in range(2)]
                    for f in range(NFF):
                        wgt, b_wg = wg[f % 3]
                        wut, b_wu = wu[f % 3]
                        P.dma("pool", lambda e, f=f, wgt=wgt: e.dma_start(
                            out=wgt[:], in_=wgu_d[:, f * 128:(f + 1) * 128].rearrange("(k p) n -> p k n", p=128)), writes=[b_wg])
                        P.dma("pool", lambda e, f=f, wut=wut: e.dma_start(
                            out=wut[:], in_=wgu_d[:, D_FF + f * 128:D_FF + (f + 1) * 128].rearrange("(k p) n -> p k n", p=128)),
                            writes=[b_wu])
                        for tg in range(TOK // 512):
                            pgt, b_pgt = pg_[(f * 2 + tg) % 2]
                            put, b_put = pu_[(f * 2 + tg) % 2]
                            s_, b_s = sl_[(f * 2 + tg) % 2]
                            P.op("pe", [(lambda e, k=k, wgt=wgt, pgt=pgt, tg=tg: e.matmul(
                                pgt[:], lhsT=wgt[:, k, :], rhs=hf2[:, k, tg * 512:(tg + 1) * 512],
                                start=(k == 0), stop=(k == KC - 1))) for k in range(KC)],
                                reads=[b_wg, b_hf2], writes=[b_pgt])
                            P.op("pe", [(lambda e, k=k, wut=wut, put=put, tg=tg: e.matmul(
                                put[:], lhsT=wut[:, k, :], rhs=hf2[:, k, tg * 512:(tg + 1) * 512],
                                start=(k == 0), stop=(k == KC - 1))) for k in range(KC)],
                                reads=[b_wu, b_hf2], writes=[b_put])
                            P.op("act", lambda e, s_=s_, pgt=pgt: e.activation(out=s_[:], in_=pgt[:], func=AF.Silu),
                                 reads=[b_pgt], writes=[b_s])
                            P.op("dve", lambda e, s_=s_, put=put, f=f, tg=tg: e.tensor_tensor(
                                out=actT[:, f, tg * 512:(tg + 1) * 512], in0=put[:], in1=s_[:], op=ALU.mult),
                                reads=[b_put, b_s], writes=[b_actT])
                    P.emit()
            with ExitStack() as es4c:
                Cc = Ctx(nc, es4c)
                DW = 512
                x1s = [Cc.sb([128, DW], F32) for _ in range(3)]
                t2 = [Cc.sb([128, DW], F32) for _ in range(2)]
                psD = [Cc.ps([128, 512], F32) for _ in range(2)]

                def ep_down(b, cb, outs):
                    pp, b_pp = outs[0]
                    xs_, b_xs = x1s[(cb * NB + b) % 3]
                    t_, b_t = t2[(cb * NB + b) % 2]
                    P.dma("sp", lambda e: e.dma_start(out=xs_[:], in_=x1_d[b * 128:(b + 1) * 128, cb * DW:(cb + 1) * DW]),
                          reads=[b_x1d[b]], writes=[b_xs])
                    P.op("dve", lambda e: e.tensor_tensor(out=t_[:], in0=pp[:, 0:DW], in1=gf_row[:, cb * DW:(cb + 1) * DW],
                                                          op=ALU.mult), reads=[b_pp, b_gf], writes=[b_t])
                    P.op("dve", lambda e: e.tensor_tensor(out=xs_[:], in0=xs_[:], in1=t_[:], op=ALU.add),
                         reads=[b_t, b_xs], writes=[b_xs])
                    P.dma("sp", lambda e: e.dma_start(out=x2_d[b * 128:(b + 1) * 128, cb * DW:(cb + 1) * DW], in_=xs_[:]),
                          reads=[b_xs], writes=[b_x2d[b][cb]])
                gemm_tok(Cc, [(actT, b_actT, (lambda cb: wd_d[:, cb * DW:(cb + 1) * DW]), NFF)], D // DW, DW, ep_down, [psD])
                P.wait_all("sp", [bb for row in b_x2d for bb in row[:D // DW]])
                P.emit()
        with ExitStack() as es5:
            C = Ctx(nc, es5)
            nfr, b_nfr = C.sb([128, D], F32)
            P.dma("sp", lambda e: e.dma_start(out=nfr[:], in_=nfin_d.partition_broadcast(128)), writes=[b_nfr])
            xb = [C.sb([128, D], F32) for _ in range(3)]
            junk, b_junk = C.sb([128, D], BF16)
            st, b_st = C.sb([128, NB, 3], F32)
            def load_x2(b):
                xt, b_xt = xb[b % 3]
                P.dma("sp", lambda e: e.dma_start(out=xt[:], in_=x2_d[b * 128:(b + 1) * 128, :]),
                      reads=b_x2d[b], writes=[b_xt])
            load_x2(0)
            for b in range(NB):
                xt, b_xt = xb[b % 3]
                if b + 1 < NB:
                    load_x2(b + 1)
                P.op("act", lambda e, b=b, xt=xt: e.activation(out=junk[:], in_=xt[:], func=AF.Square, accum_out=st[:, b, 0:1]),
                     reads=[b_xt], writes=[b_junk, b_st])
                P.op("act", lambda e, b=b: e.activation(out=st[:, b, 1:2], in_=st[:, b, 0:1], func=AF.Sqrt, scale=1.0 / D, bias=EPS),
                     reads=[b_st], writes=[b_st])
                P.op("dve", lambda e, b=b: e.reciprocal(out=st[:, b, 2:3], in_=st[:, b, 1:2]), reads=[b_st], writes=[b_st])
                P.op("dve", lambda e, b=b, xt=xt: e.scalar_tensor_tensor(out=xt[:], in0=xt[:], scalar=st[:, b, 2:3], in1=nfr[:],
                                                                         op0=ALU.mult, op1=ALU.mult),
                     reads=[b_st, b_xt, b_nfr], writes=[b_xt])
                ob = Buf()
                outbufs.append(ob)
                P.dma("sp", lambda e, xt=xt, b=b: e.dma_start(out=out_d[b * 128:(b + 1) * 128, :], in_=xt[:]),
                      reads=[b_xt], writes=[ob])
            P.wait_all("sp", outbufs)
            P.emit()
    return nc


def prep_C_inputs(inp, r, HAF, HAB, OBF, OBB, modfm):
    w_in = inp["w_in"][0]
    sl = slice(r * TOK, (r + 1) * TOK)
    w3 = np.concatenate([w_in[:, 4096:6144], w_in[:, 12304:14352], w_in[:, 14416:18512]], axis=1)
    return {
        "x": np.ascontiguousarray(inp["x"][0][sl]),
        "modfm": modfm,
        "norm_mix_w": np.ascontiguousarray(inp["norm_mix_w"][0]),
        "norm_ffn_w": np.ascontiguousarray(inp["norm_ffn_w"][0]),
        "norm_final_w": np.ascontiguousarray(inp["norm_final_w"]),
        "mlstm_norm_w": np.ascontiguousarray(inp["mlstm_norm_w"][0]),
        "gdn_norm_w": np.ascontiguousarray(inp["gdn_norm_w"][0]),
        "w3": np.ascontiguousarray(w3),
        "w_branch_a": np.ascontiguousarray(inp["w_branch_a"][0]),
        "w_branch_b": np.ascontiguousarray(inp["w_branch_b"][0]),
        "w_out": np.ascontiguousarray(inp["w_out"][0]),
        "w_gate_up": np.ascontiguousarray(inp["w_gate_up"][0]),
        "w_down": np.ascontiguousarray(inp["w_down"][0]),
        "haf": np.ascontiguousarray(HAF[sl]),
        "hab": np.ascontiguousarray(HAB[sl]),
        "obf": np.ascontiguousarray(OBF[sl]),
        "obb": np.ascontiguousarray(OBB[sl]),
    }


def build_AB():
    nc = bass.Bass("TRN2", target_bir_lowering=False)
    with ExitStack() as es:
        sems = [es.enter_context(nc.semaphore(f"s{i}")) for i in range(100)]
        sh = {"nc": nc, "P": Prog(nc, sems), "scr": {}}
        build_A(sh=sh)
        build_B(sh=sh)
    return nc


def kernel(**inputs):
    inp = {k: np.asarray(v) for k, v in inputs.items()}
    cores = list(range(NCORE))
    ncAB = build_AB()
    resB = run_bass_kernel_spmd(ncAB, [prep_A_inputs(inp, r) for r in cores], core_ids=cores)
    HAF = np.concatenate([np.asarray(resB.results[r]["hab"])[:, 0:256] for r in cores], axis=1)
    HAB = np.concatenate([np.asarray(resB.results[r]["hab"])[:, 256:512] for r in cores], axis=1)
    OBF = np.concatenate([np.asarray(resB.results[r]["hab"])[:, 512:768] for r in cores], axis=1)
    OBB = np.concatenate([np.asarray(resB.results[r]["hab"])[:, 768:1024] for r in cores], axis=1)
    mo = [np.asarray(resB.results[r]["modout"]) for r in cores]
    modfm = np.ascontiguousarray(np.concatenate([mo[0][:, 0:32]] + [m[:, 32:40] for m in mo], axis=1))
    ncC = build_C()
    resC = run_bass_kernel_spmd(ncC, [prep_C_inputs(inp, r, HAF, HAB, OBF, OBB, modfm) for r in cores], core_ids=cores)
    out = np.concatenate([np.asarray(resC.results[r]["out"]) for r in cores], axis=0)
    return out.reshape(1, T, D).astype(np.float32)
```
